# Optimizing a Trainium2 kernel written in Bass

```python
import jax, jax.numpy as jnp
from jax import lax
import numpy as np

D_MODEL = 2048
BATCH = 1
SEQ = 8192
DEPTH = 4

BRANCH_WIDTH = D_MODEL // 2
N_BRANCH = 3
MLA_NOPE = 128
MLA_ROPE = 64
MLA_V = 128
MLA_HEADS = BRANCH_WIDTH // MLA_V
MLA_Q_RANK = 512
MLA_KV_RANK = 512
ROPE_THETA = 10000.0
GLA_HEADS = 4
GLA_DV = BRANCH_WIDTH // GLA_HEADS
GLA_DK = GLA_DV // 2
GLA_GATE_RANK = 16
GLA_TAU = 16.0
GLA_CHUNK = 64
FOX_DH = 128
FOX_HEADS = BRANCH_WIDTH // FOX_DH
FORGET_BIAS_INIT = 3.0
FFN_HIDDEN = ((8 * D_MODEL + 3 * 256 - 1) // (3 * 256)) * 256
Q_BLOCK = 128
EPS = 1e-6
NEG_INF = -1e30

SPLIT_SIZES = (
    MLA_Q_RANK,
    MLA_KV_RANK,
    MLA_ROPE,
    GLA_HEADS * GLA_DK,
    GLA_HEADS * GLA_DK,
    GLA_HEADS * GLA_DV,
    GLA_GATE_RANK,
    GLA_HEADS * GLA_DV,
    FOX_HEADS * FOX_DH,
    FOX_HEADS * FOX_DH,
    FOX_HEADS * FOX_DH,
    FOX_HEADS,
    N_BRANCH * D_MODEL,
)
D_IN = sum(SPLIT_SIZES)
SPLIT_IDX = tuple(int(v) for v in np.cumsum(SPLIT_SIZES)[:-1])

kernel_name = "hybrid_mla_gla_fox_gated_block"


def rms_norm(x, g):
    xf = x.astype(jnp.float32)
    y = xf * lax.rsqrt(jnp.mean(xf * xf, axis=-1, keepdims=True) + EPS)
    return (y * g.astype(jnp.float32)).astype(x.dtype)


def rope(x, pos):
    half = x.shape[-1] // 2
    inv = ROPE_THETA ** (-jnp.arange(half, dtype=jnp.float32) / half)
    ang = pos.astype(jnp.float32)[:, :, None] * inv
    cos = jnp.cos(ang)[:, :, None, :]
    sin = jnp.sin(ang)[:, :, None, :]
    x1 = x[..., :half].astype(jnp.float32)
    x2 = x[..., half:].astype(jnp.float32)
    return jnp.concatenate([x1 * cos - x2 * sin, x2 * cos + x1 * sin], axis=-1).astype(x.dtype)


def block_causal_attention(q, k, v, scale, log_f_cum=None):
    B, S, H, dk = q.shape
    nb = S // Q_BLOCK
    q_blocks = q.reshape(B, nb, Q_BLOCK, H, dk).swapaxes(0, 1)
    key_pos = jnp.arange(S)
    use_forget = log_f_cum is not None
    if use_forget:
        f_keys = log_f_cum.astype(jnp.float32).swapaxes(1, 2)
        f_blocks = log_f_cum.astype(jnp.float32).reshape(B, nb, Q_BLOCK, H).swapaxes(0, 1)
        xs = (jnp.arange(nb), q_blocks, f_blocks)
    else:
        xs = (jnp.arange(nb), q_blocks)

    def one_block(args):
        i, q_blk = args[0], args[1]
        s = jnp.einsum('bqhd,bkhd->bhqk', q_blk, k, preferred_element_type=jnp.float32) * scale
        if use_forget:
            s = s + args[2].swapaxes(1, 2)[..., None] - f_keys[:, :, None, :]
        q_pos = i * Q_BLOCK + jnp.arange(Q_BLOCK)
        s = jnp.where(key_pos[None, :] <= q_pos[:, None], s, NEG_INF)
        p = jax.nn.softmax(s, axis=-1).astype(v.dtype)
        return jnp.einsum('bhqk,bkhd->bqhd', p, v)

    out = lax.map(one_block, xs)
    return out.swapaxes(0, 1).reshape(B, S, H, v.shape[-1])


def gla_chunked(q, k, v, log_a):
    B, S, H, dk = q.shape
    dv = v.shape[-1]
    C = GLA_CHUNK
    n = S // C

    def to_chunks(t):
        return t.astype(jnp.float32).reshape(B, n, C, H, t.shape[-1]).transpose(1, 0, 3, 2, 4)

    qc, kc, vc, ac = to_chunks(q), to_chunks(k), to_chunks(v), to_chunks(log_a)
    causal = jnp.tril(jnp.ones((C, C), dtype=bool))[:, :, None]

    def step(state, inp):
        qi, ki, vi, ai = inp
        b = jnp.cumsum(ai, axis=-2)
        diff = b[:, :, :, None, :] - b[:, :, None, :, :]
        decay = jnp.exp(jnp.where(causal, diff, -jnp.inf))
        attn = jnp.einsum('bhid,bhjd,bhijd->bhij', qi, ki, decay)
        o = (jnp.einsum('bhij,bhjv->bhiv', attn, vi)
             + jnp.einsum('bhid,bhdv->bhiv', qi * jnp.exp(b), state))
        b_last = b[:, :, -1:, :]
        new_state = (jnp.exp(b_last[:, :, 0, :])[..., None] * state
                     + jnp.einsum('bhjd,bhjv->bhdv', ki * jnp.exp(b_last - b), vi))
        return new_state, o

    init = jnp.zeros((B, H, dk, dv), jnp.float32)
    _, o = lax.scan(step, init, (qc, kc, vc, ac))
    return o.transpose(1, 0, 3, 2, 4).reshape(B, S, H, dv)


def hybrid_layer(x, pos, g_mix, w_in, g_cq, w_uq, g_ckv, w_ukv, g_mla_q, g_mla_k,
                 w_a2, b_a, g_gla_o, g_fox_q, g_fox_k, b_f, w_branch, w_out,
                 g_ffn, w_gu, w_down):
    B, S, _ = x.shape
    h = rms_norm(x, g_mix)
    proj = h @ w_in
    (cq, ckv, kr, gq, gk, gv, ga, gr, fq, fk, fv, fl, gates) = jnp.split(proj, SPLIT_IDX, axis=-1)

    q_a = (rms_norm(cq, g_cq) @ w_uq).reshape(B, S, MLA_HEADS, MLA_NOPE + MLA_ROPE)
    kv_a = (rms_norm(ckv, g_ckv) @ w_ukv).reshape(B, S, MLA_HEADS, MLA_NOPE + MLA_V)
    k_nope, v_a = kv_a[..., :MLA_NOPE], kv_a[..., MLA_NOPE:]
    k_a = jnp.concatenate([k_nope, jnp.broadcast_to(kr[:, :, None, :], (B, S, MLA_HEADS, MLA_ROPE))], axis=-1)
    q_a = rms_norm(q_a, g_mla_q)
    k_a = rms_norm(k_a, g_mla_k)
    q_a = jnp.concatenate([q_a[..., :MLA_NOPE], rope(q_a[..., MLA_NOPE:], pos)], axis=-1)
    k_a = jnp.concatenate([k_a[..., :MLA_NOPE], rope(k_a[..., MLA_NOPE:], pos)], axis=-1)
    o_a = block_causal_attention(q_a, k_a, v_a, (MLA_NOPE + MLA_ROPE) ** -0.5)

    q_b = gq.reshape(B, S, GLA_HEADS, GLA_DK) * (GLA_DK ** -0.5)
    k_b = gk.reshape(B, S, GLA_HEADS, GLA_DK)
    v_b = gv.reshape(B, S, GLA_HEADS, GLA_DV)
    log_a = (jax.nn.log_sigmoid((ga @ w_a2 + b_a).astype(jnp.float32)) / GLA_TAU).reshape(B, S, GLA_HEADS, GLA_DK)
    o_b = gla_chunked(q_b, k_b, v_b, log_a).astype(x.dtype)
    o_b = rms_norm(o_b, g_gla_o) * jax.nn.silu(gr.reshape(B, S, GLA_HEADS, GLA_DV))

    q_c = rms_norm(fq.reshape(B, S, FOX_HEADS, FOX_DH), g_fox_q)
    k_c = rms_norm(fk.reshape(B, S, FOX_HEADS, FOX_DH), g_fox_k)
    v_c = fv.reshape(B, S, FOX_HEADS, FOX_DH)
    log_f_cum = jnp.cumsum(jax.nn.log_sigmoid((fl + b_f).astype(jnp.float32)), axis=1)
    o_c = block_causal_attention(q_c, k_c, v_c, FOX_DH ** -0.5, log_f_cum)

    branches = jnp.stack([o_a.reshape(B, S, BRANCH_WIDTH), o_b.reshape(B, S, BRANCH_WIDTH),
                          o_c.reshape(B, S, BRANCH_WIDTH)], axis=2)
    y = jnp.einsum('bsnc,ncd->bsnd', branches, w_branch)
    g = jax.nn.sigmoid(gates.reshape(B, S, N_BRANCH, D_MODEL))
    merged = jnp.einsum('bsnd,bsnd->bsd', y, g)
    x = x + merged @ w_out

    h2 = rms_norm(x, g_ffn)
    gate, up = jnp.split(h2 @ w_gu, 2, axis=-1)
    return x + (jax.nn.silu(gate) * up) @ w_down


def setup_inputs(seed: int = 0) -> dict:
    key = jax.random.key(seed)
    ks = jax.random.split(key, 24)
    f32 = jnp.float32

    def w(k, shape, fan_in):
        return jax.random.normal(k, shape, f32) * (fan_in ** -0.5)

    def gain(k, shape):
        return 1.0 + 0.02 * jax.random.normal(k, shape, f32)

    L = DEPTH
    x = jax.random.normal(ks[0], (BATCH, SEQ, D_MODEL), f32)
    offset = jax.random.randint(ks[1], (BATCH, 1), 0, SEQ, dtype=jnp.int32)
    positions = (jnp.arange(SEQ, dtype=jnp.int32)[None, :] + offset).astype(jnp.int32)
    return {
        "x": x,
        "positions": positions,
        "g_mix": gain(ks[2], (L, D_MODEL)),
        "w_in": w(ks[3], (L, D_MODEL, D_IN), D_MODEL),
        "g_cq": gain(ks[4], (L, MLA_Q_RANK)),
        "w_uq": w(ks[5], (L, MLA_Q_RANK, MLA_HEADS * (MLA_NOPE + MLA_ROPE)), MLA_Q_RANK),
        "g_ckv": gain(ks[6], (L, MLA_KV_RANK)),
        "w_ukv": w(ks[7], (L, MLA_KV_RANK, MLA_HEADS * (MLA_NOPE + MLA_V)), MLA_KV_RANK),
        "g_mla_q": gain(ks[8], (L, MLA_NOPE + MLA_ROPE)),
        "g_mla_k": gain(ks[9], (L, MLA_NOPE + MLA_ROPE)),
        "w_a2": w(ks[10], (L, GLA_GATE_RANK, GLA_HEADS * GLA_DK), GLA_GATE_RANK),
        "b_a": 0.1 * jax.random.normal(ks[11], (L, GLA_HEADS * GLA_DK), f32),
        "g_gla_o": gain(ks[12], (L, GLA_DV)),
        "g_fox_q": gain(ks[13], (L, FOX_DH)),
        "g_fox_k": gain(ks[14], (L, FOX_DH)),
        "b_f": FORGET_BIAS_INIT + 0.1 * jax.random.normal(ks[15], (L, FOX_HEADS), f32),
        "w_branch": w(ks[16], (L, N_BRANCH, BRANCH_WIDTH, D_MODEL), BRANCH_WIDTH),
        "w_out": w(ks[17], (L, D_MODEL, D_MODEL), D_MODEL),
        "g_ffn": gain(ks[18], (L, D_MODEL)),
        "w_gu": w(ks[19], (L, D_MODEL, 2 * FFN_HIDDEN), D_MODEL),
        "w_down": w(ks[20], (L, FFN_HIDDEN, D_MODEL), FFN_HIDDEN),
    }


def reference(x, positions, g_mix, w_in, g_cq, w_uq, g_ckv, w_ukv, g_mla_q, g_mla_k,
              w_a2, b_a, g_gla_o, g_fox_q, g_fox_k, b_f, w_branch, w_out,
              g_ffn, w_gu, w_down):
    h = x
    for l in range(DEPTH):
        h = hybrid_layer(h, positions, g_mix[l], w_in[l], g_cq[l], w_uq[l], g_ckv[l], w_ukv[l],
                         g_mla_q[l], g_mla_k[l], w_a2[l], b_a[l], g_gla_o[l], g_fox_q[l],
                         g_fox_k[l], b_f[l], w_branch[l], w_out[l], g_ffn[l], w_gu[l], w_down[l])
    return h
```

```python
import contextlib
import math
import numpy as np
import ml_dtypes
import concourse.bass as bass
import concourse.mybir as mybir
from concourse.bass_utils import run_bass_kernel_spmd

F32 = mybir.dt.float32
BF16 = mybir.dt.bfloat16
I32 = mybir.dt.int32
AF = mybir.ActivationFunctionType
ALU = mybir.AluOpType
AX = mybir.AxisListType

NCORES = 8
DEPTH = 4
D = 2048
KC = 16
TOK = 1024
TT = 512
DIN = 13400
HID = 5632
EPS = 1e-6
C_CQ, C_CKV, C_KR, C_GQ, C_GK, C_GV, C_GA, C_GR, C_FQ, C_FK, C_FV, C_FL, C_GATES = (
    0, 512, 1024, 1088, 1600, 2112, 3136, 3152, 4176, 5200, 6224, 7248, 7256)
R_KA, R_VA, R_KC, R_VC, R16 = 0, 1536, 2560, 3584, 4608
R_LF, R_UX, R_DX, R32 = 0, 8, 1032, 1036
G_MIX, G_FFN, G_CQ, G_CKV, G_QN, G_QR, G_QRS, G_KN, G_KR, G_KRS, G_GLA, G_FQ, G_FK, G_BF, GL = (
    0, 16, 32, 36, 40, 41, 42, 43, 44, 45, 46, 48, 49, 50, 58)
FUSED = False


class V:
    __slots__ = ("ap", "buf")

    def __init__(self, ap, buf):
        self.ap = ap
        self.buf = buf


class Buf:
    def __init__(self, t, name):
        self.t = t
        self.name = name
        self.w = {}
        self.r = {}
        self.dkey = None
        self.multi = False
        self.psum = False

    def __getitem__(self, idx):
        return V(self.t[idx], self)

    def v(self, ap):
        return V(ap, self)


class WView:
    def __init__(self, buf, ap):
        self.buf = buf
        self.ap3 = ap

    def __getitem__(self, idx):
        return V(self.ap3[idx], self.buf)


class KB:
    def __init__(self, nc, es):
        self.nc = nc
        self.es = es
        self.E = dict(pe=nc.tensor, act=nc.scalar, dve=nc.vector, pool=nc.gpsimd, sp=nc.sync)
        self.sem = {}
        self.cnt = {}
        for e in list(self.E) + ["cc"]:
            self.sem[e] = es.enter_context(nc.semaphore("s_" + e))
            self.cnt[e] = 0
        self.known = {e: {} for e in self.E}
        self.nd = 0
        self.nbuf = 0
        self.psums = []
        self.psi = 0
        self.stopped = False

    def sbuf(self, name, shape, dtype, es=None):
        self.nbuf += 1
        t = (es or self.es).enter_context(self.nc.sbuf_tensor("%s_%d" % (name, self.nbuf), shape, dtype))
        return Buf(t, name)

    def dram(self, name, shape, dtype):
        t = self.nc.dram_tensor(name, shape, dtype)
        return Buf(t, name)

    def init_psum(self, nrot):
        for i in range(8):
            t = self.es.enter_context(self.nc.psum_tensor("ps%d" % i, [128, 512], F32))
            self.psums.append(Buf(t, "ps%d" % i))
            self.psums[-1].psum = True
        self.nrot = nrot

    def ps(self):
        b = self.psums[self.psi % self.nrot]
        self.psi += 1
        return b

    def _dkey(self, buf):
        if buf.dkey is None:
            self.nd += 1
            key = "d%d" % self.nd
            self.sem[key] = self.es.enter_context(self.nc.semaphore(key))
            self.cnt[key] = 0
            buf.dkey = key
        return buf.dkey

    def _wait(self, eng, key, val):
        if val <= self.known[eng].get(key, 0):
            return
        self.known[eng][key] = val
        self.E[eng].wait_ge(self.sem[key], val)

    def _deps(self, eng, rb, wb, extra=()):
        deps = {}

        def add(tok):
            if tok is None:
                return
            k_, v_ = tok
            if eng == "pe" and k_ == "pe":
                return
            if deps.get(k_, 0) < v_:
                deps[k_] = v_

        for b in rb:
            for k_, v_ in b.w.items():
                add((k_, v_))
            if b.psum:
                for k_, v_ in b.r.items():
                    if k_ != eng:
                        add((k_, v_))
        for b in wb:
            if b.multi:
                continue
            for k_, v_ in b.w.items():
                add((k_, v_))
            for k_, v_ in b.r.items():
                add((k_, v_))
        for t in extra:
            add(t)
        for k_, v_ in deps.items():
            self._wait(eng, k_, v_)

    def _commit(self, tok, rb, wb):
        for b in wb:
            if b.multi:
                if b.w.get(tok[0], 0) < tok[1]:
                    b.w[tok[0]] = tok[1]
            else:
                b.w = {tok[0]: tok[1]}
                b.r = {}
        for b in rb:
            if b in wb:
                continue
            if b.r.get(tok[0], 0) < tok[1]:
                b.r[tok[0]] = tok[1]

    @staticmethod
    def _bufs(vs):
        out = []
        for v in vs:
            if v is None or isinstance(v, (int, float)):
                continue
            if v.buf is not None and v.buf not in out:
                out.append(v.buf)
        return out

    def op(self, eng, fn, reads, writes):
        if self.stopped:
            return
        rb = self._bufs(reads)
        wb = self._bufs(writes)
        self._deps(eng, rb, wb)
        inst = fn(self.E[eng])
        self.cnt[eng] += 1
        inst.then_inc(self.sem[eng], 1)
        self._commit((eng, self.cnt[eng]), rb, wb)

    def dma(self, q, out, in_, sembuf=None):
        if self.stopped:
            return
        if sembuf is not None:
            sb = sembuf
        elif out.buf is not None and not (out.buf.multi and in_.buf is not None):
            sb = out.buf
        else:
            sb = in_.buf
        key = self._dkey(sb)
        rb = self._bufs([in_])
        wb = self._bufs([out])
        prev = (key, self.cnt[key]) if self.cnt[key] else None
        self._deps(q, rb, wb, extra=(prev,))
        self.E[q].dma_start(out=out.ap, in_=in_.ap).then_inc(self.sem[key], 16)
        self.cnt[key] += 16
        self._commit((key, self.cnt[key]), rb, wb)

    def allgather(self, send, recv):
        if self.stopped:
            return
        rb = [send]
        wb = [recv]
        self._deps("pool", rb, wb)
        self.nc.gpsimd.collective_compute(
            "AllGather", ALU.bypass, replica_groups=[list(range(NCORES))],
            ins=[send.t.ap().opt()], outs=[recv.t.ap().opt()]).then_inc(self.sem["cc"])
        self.cnt["cc"] += 1
        self._commit(("cc", self.cnt["cc"]), rb, wb)

    def barrier(self, engines=None):
        if self.stopped:
            return
        for e in (engines or self.E):
            for key, c in self.cnt.items():
                if c:
                    self._wait(e, key, c)

    def mm(self, out, lhsT, rhs, start, stop):
        self.op("pe", lambda E: E.matmul(out.ap, lhsT=lhsT.ap, rhs=rhs.ap, start=start, stop=stop),
                [lhsT, rhs] + ([] if start else [out]), [out])

    def act(self, out, in_, func, bias=None, scale=1.0):
        kw = {}
        if bias is not None:
            kw["bias"] = bias.ap if isinstance(bias, V) else bias
        self.op("act", lambda E: E.activation(out=out.ap, in_=in_.ap, func=func, scale=scale, **kw),
                [in_, bias], [out])

    def tt(self, out, in0, in1, op, eng="dve"):
        self.op(eng, lambda E: E.tensor_tensor(out=out.ap, in0=in0.ap, in1=in1.ap, op=op), [in0, in1], [out])

    def ts(self, out, in0, s1, op0, s2=None, op1=None, eng="dve"):
        a1 = s1.ap if isinstance(s1, V) else s1
        a2 = s2.ap if isinstance(s2, V) else s2
        if op1 is None:
            fn = lambda E: E.tensor_scalar(out=out.ap, in0=in0.ap, scalar1=a1, scalar2=None, op0=op0)
        else:
            fn = lambda E: E.tensor_scalar(out=out.ap, in0=in0.ap, scalar1=a1, scalar2=a2, op0=op0, op1=op1)
        self.op(eng, fn, [in0, s1, s2], [out])

    def stt(self, out, in0, scalar, in1, op0, op1):
        a = scalar.ap if isinstance(scalar, V) else scalar
        self.op("dve", lambda E: E.scalar_tensor_tensor(out=out.ap, in0=in0.ap, scalar=a, in1=in1.ap,
                                                        op0=op0, op1=op1), [in0, scalar, in1], [out])

    def copy(self, out, in_, eng="dve"):
        if eng == "act":
            self.op(eng, lambda E: E.activation(out=out.ap, in_=in_.ap, func=AF.Copy), [in_], [out])
        else:
            self.op(eng, lambda E: E.tensor_copy(out=out.ap, in_=in_.ap), [in_], [out])

    def memset(self, out, val, eng="dve"):
        self.op(eng, lambda E: E.memset(out.ap, val), [], [out])

    def recip(self, out, in_):
        self.op("dve", lambda E: E.reciprocal(out=out.ap, in_=in_.ap), [in_], [out])


class _Stop(Exception):
    pass


class _Dummy:
    def __getitem__(self, idx):
        return self

    def rearrange(self, *a, **kw):
        return self

    def ap(self):
        return self

    def opt(self):
        return self


def build(nlayers, mode="fused"):
    import os
    stop_at = os.environ.get("K_STOP", "")

    kref = []

    def ck(name):
        if stop_at == name and not kref[0].stopped:
            kref[0].barrier()
            kref[0].stopped = True
    nc = bass.Bass("TRN2", target_bir_lowering=False)
    es = contextlib.ExitStack()
    with es:
        k = KB(nc, es)
        kref.append(k)

        def ext_in(name, shape, dt):
            return Buf(nc.dram_tensor(name, shape, dt, kind="ExternalInput"), name)

        xT_d = ext_in("xT", [128, KC, TOK], F32)
        pos_d = ext_in("pos", [64, TOK], I32)
        gains_d = ext_in("gains", [128, DEPTH * GL], F32)
        mk_d = ext_in("mk", [128, 8, 128], BF16)
        tri_d = ext_in("tri", [128, 128], BF16)
        tris_d = ext_in("tris", [128, 128], F32)
        tril1_d = ext_in("tril1", [128, 128], F32)
        cvec_d = ext_in("cvec", [128, 20], F32)
        def wdecl(name, shape, used):
            if used:
                return ext_in(name, shape, F32)
            return Buf(_Dummy(), name)
        isA, isB, isF = mode == "A", mode == "B", mode == "fused"
        if isA:
            cols = dict(CKV=0, KR=512, GK=576, GV=1088, GA=2112, FK=2128, FV=3152, FL=4176)
            w_ink_d = ext_in("w_ink", [nlayers, D, 4184], F32)
        else:
            cols = dict(CKV=C_CKV, KR=C_KR, GK=C_GK, GV=C_GV, GA=C_GA, FK=C_FK, FV=C_FV, FL=C_FL)
        w_in_d = wdecl("w_in", [nlayers, D, DIN], not isA)
        if not isA:
            w_ink_d = w_in_d
        w_uq_d = wdecl("w_uq", [nlayers, 512, 1536], not isA)
        w_ukv_d = wdecl("w_ukv", [nlayers, 512, 2048], not isB)
        w_a2_d = ext_in("w_a2aug", [nlayers, 17, 512], F32)
        w_br_d = wdecl("w_branch", [nlayers, 3, 1024, D], not isA)
        w_out_d = wdecl("w_out", [nlayers, D, D], not isA)
        w_gu_d = wdecl("w_gu", [nlayers, D, 2 * HID], not isA)
        w_dn_d = wdecl("w_down", [nlayers, HID, D], not isA)
        if not isA:
            out_d = Buf(nc.dram_tensor("outT", [128, KC, TOK], F32, kind="ExternalOutput"), "outT")
        for b in (xT_d, pos_d, gains_d, mk_d, tri_d, tris_d, tril1_d, cvec_d):
            pass
        WIN = [w_in_d.t[l].rearrange("(kc p) n -> p kc n", p=128) for l in range(nlayers)]
        WINK = [w_ink_d.t[l].rearrange("(kc p) n -> p kc n", p=128) for l in range(nlayers)]
        WUQ = [w_uq_d.t[l].rearrange("(kc p) n -> p kc n", p=128) for l in range(nlayers)]
        WUKV = [w_ukv_d.t[l].rearrange("(kc p) n -> p kc n", p=128) for l in range(nlayers)]
        WBR = [[w_br_d.t[l, n].rearrange("(kc p) n -> p kc n", p=128) for n in range(3)] for l in range(nlayers)]
        WOUT = [w_out_d.t[l].rearrange("(kc p) n -> p kc n", p=128) for l in range(nlayers)]
        WGU = [w_gu_d.t[l].rearrange("(kc p) n -> p kc n", p=128) for l in range(nlayers)]
        WDN = [w_dn_d.t[l].rearrange("(kc p) n -> p kc n", p=128) for l in range(nlayers)]

        def xbuf(name, shape, dt, kind):
            if kind == "none":
                return Buf(_Dummy(), name)
            if kind is None:
                return k.dram(name, shape, dt)
            return Buf(nc.dram_tensor(name, shape, dt, kind=kind), name)
        sk = "none" if isB else None
        gk_ = "ExternalInput" if isB else ("none" if isA else None)
        S16 = [xbuf("s16_%d" % l, [R16, 1024], BF16, sk) for l in range(nlayers)]
        G16 = [xbuf("g16_%d" % l, [8 * R16, 1024], BF16, gk_) for l in range(nlayers)]
        S32 = [xbuf("s32_%d" % l, [R32, 1024], F32, sk) for l in range(nlayers)]
        G32 = [xbuf("g32_%d" % l, [8 * R32, 1024], F32, gk_) for l in range(nlayers)]
        for b in S16 + S32:
            b.multi = True

        k.init_psum(6)
        PSO = k.psums[6]
        PSD = k.psums[7]

        XT = k.sbuf("XT", [128, KC, TOK], F32)
        GN = k.sbuf("GN", [128, DEPTH * GL], F32)
        MK = k.sbuf("MK", [128, 8, 128], BF16)
        TRI = k.sbuf("TRI", [128, 128], BF16)
        TRIS = k.sbuf("TRIS", [128, 128], F32)
        TRIL1 = k.sbuf("TRIL1", [128, 128], F32)
        CVEC = k.sbuf("CVEC", [128, 20], F32)
        ONESB = k.sbuf("ONESB", [128, 128], BF16)
        ONESF = k.sbuf("ONESF", [128, 128], F32)
        ROPC = k.sbuf("ROPC", [64, TOK], F32)
        ROPS = k.sbuf("ROPS", [64, TOK], F32)
        WR = [k.sbuf("WR%d" % i, [128, 4096], BF16) for i in range(3)]
        SQ = [k.sbuf("SQ%d" % i, [128, TT], BF16) for i in range(2)]
        RV = [k.sbuf("RV%d" % i, [128, TT], F32) for i in range(3)]
        state = dict(wi=0, sq=0, rv=0)

        def sqb():
            state["sq"] += 1
            return SQ[state["sq"] % 2]

        def rvb():
            state["rv"] += 1
            return RV[state["rv"] % 3]

        def wload(src_ap, kcn, n, split=None):
            slot = WR[state["wi"] % 3]
            state["wi"] += 1
            v3 = slot.t[:, 0:kcn * n].rearrange("p (k n) -> p k n", k=kcn)
            if split is None:
                k.dma("pool", V(v3, slot), V(src_ap, None))
            else:
                dst = slot.t[:, 0:kcn * n].rearrange("p (k h c) -> p k h c", k=kcn, h=split)
                for kc in range(kcn):
                    k.dma("pool", V(dst[:, kc], slot), V(src_ap[:, kc], None))
            return WView(slot, v3)

        k.dma("sp", XT[:, :, :], xT_d[:, :, :])
        k.dma("sp", GN[:, :], gains_d[:, :])
        k.dma("sp", MK[:, :, :], mk_d[:, :, :])
        k.dma("sp", TRI[:, :], tri_d[:, :])
        k.dma("sp", TRIS[:, :], tris_d[:, :])
        k.dma("sp", TRIL1[:, :], tril1_d[:, :])
        k.dma("sp", CVEC[:, :], cvec_d[:, :])
        k.memset(ONESB[:, :], 1.0)
        k.memset(ONESF[:, :], 1.0)
        MASKR = lambda r: CVEC[:, r:r + 1]
        OH = lambda r: CVEC[:, 8 + r:9 + r]

        with contextlib.ExitStack() as s0:
            POSI = k.sbuf("POSI", [64, TOK], I32, s0)
            ANG = k.sbuf("ANG", [64, TOK], F32, s0)
            T1 = k.sbuf("T1", [64, TOK], F32, s0)
            T2 = k.sbuf("T2", [64, TOK], F32, s0)
            k.dma("sp", POSI[:, :], pos_d[:, :])
            k.copy(ANG[:, :], POSI[:, :])
            k.ts(ANG[:, :], ANG[:, :], CVEC[0:64, 16:17], ALU.mult)
            MAGIC = 12582912.0
            C1 = 6.28125
            C2 = 2.0 * math.pi - 6.28125
            for which, dst in ((0, ROPS), (1, ROPC)):
                k.ts(T1[:, :], ANG[:, :], 1.0 / (2.0 * math.pi), ALU.mult, (0.25 if which else 0.0), ALU.add)
                k.ts(T1[:, :], T1[:, :], MAGIC, ALU.add)
                k.ts(T1[:, :], T1[:, :], -MAGIC, ALU.add)
                k.stt(T2[:, :], T1[:, :], -C1, ANG[:, :], ALU.mult, ALU.add)
                k.stt(T2[:, :], T1[:, :], -C2, T2[:, :], ALU.mult, ALU.add)
                if which:
                    k.ts(T2[:, :], T2[:, :], math.pi / 2.0, ALU.add)
                k.ts(T2[:, :], T2[:, :], 3.1415925, ALU.min, -3.1415925, ALU.max)
                k.act(dst[:, :], T2[:, :], AF.Sin)
            k.ts(ROPS[:, :], ROPS[:, :], CVEC[0:64, 17:18], ALU.mult)
            k.barrier()

        def rinv_of(parts, nfeat, out):
            ss = k.ps()
            n = len(parts)
            for i, (p, rows, pre) in enumerate(parts):
                if pre:
                    s = p
                else:
                    sq = sqb()
                    k.act(sq[0:rows, :], p, AF.Square)
                    s = sq[0:rows, :]
                k.mm(ss[:, :], ONESB[0:rows, :], s, start=(i == 0), stop=(i == n - 1))
            k.act(out[:, :], ss[:, :], AF.Ln, bias=EPS, scale=1.0 / nfeat)
            k.act(out[:, :], out[:, :], AF.Exp, scale=-0.5)

        def rmsnorm_x(l, tt, gcol, HT):
            tsl = slice(tt * TT, (tt + 1) * TT)
            ss = k.ps()
            for kc in range(KC):
                sq = sqb()
                k.act(sq[:, :], XT[:, kc, tsl], AF.Square)
                k.mm(ss[:, :], ONESB[:, :], sq[:, :], start=(kc == 0), stop=(kc == KC - 1))
            rs = rvb()
            k.act(rs[:, :], ss[:, :], AF.Ln, bias=EPS, scale=1.0 / D)
            k.act(rs[:, :], rs[:, :], AF.Exp, scale=-0.5)
            for kc in range(KC):
                k.stt(HT[:, kc, :], XT[:, kc, tsl], GN[:, l * GL + gcol + kc:l * GL + gcol + kc + 1], rs[:, :],
                      ALU.mult, ALU.mult)

        def proj_fm(wsrc, col0, ncols, kcn, rhs_fn, consume):
            for p0 in range(0, ncols, 256):
                pn = min(256, ncols - p0)
                W = wload(wsrc[:, :, col0 + p0:col0 + p0 + pn], kcn, pn)
                for c in range(0, pn, 128):
                    m = min(128, pn - c)
                    ps = k.ps()
                    for kc in range(kcn):
                        k.mm(ps[0:m, :], W[:, kc, c:c + m], rhs_fn(kc), start=(kc == 0), stop=(kc == kcn - 1))
                    consume((p0 + c) // 128, ps, m)

        def proj_tm(wsrc_fn, ncols, kcn, lhs_fn, consume):
            for p0 in range(0, ncols, 256):
                pn = min(256, ncols - p0)
                W = wload(wsrc_fn(p0, pn), kcn, pn)
                for blk in range(4):
                    ps = k.ps()
                    for kc in range(kcn):
                        k.mm(ps[:, 0:pn], lhs_fn(kc, blk), W[:, kc, 0:pn], start=(kc == 0), stop=(kc == kcn - 1))
                    consume(blk, p0, pn, ps)

        def gcol(l, c, rows=128):
            return GN[0:rows, l * GL + c:l * GL + c + 1]

        def gla_sp(l, HT, SPB, GAA):
            def cons(ci, ps, m):
                k.copy(GAA[0:16, :], ps[0:16, :])
            proj_fm(WINK[l], cols["GA"], 16, KC, lambda kc: HT[:, kc, :], cons)
            for blk in range(4):
                z = k.ps()
                k.mm(z[:, :], GAA[0:17, blk * 128:(blk + 1) * 128], WA2[0:17, :], start=True, stop=True)
                k.act(SPB[:, blk, :], z[:, :], AF.Exp, scale=-1.0)
                k.act(SPB[:, blk, :], SPB[:, blk, :], AF.Ln, bias=1.0)

        def gla_v(l, HT, VB):
            def cons(blk, p0, pn, ps):
                k.copy(VB[:, blk, p0:p0 + pn], ps[:, 0:pn], eng="act" if blk % 2 else "dve")
            proj_tm(lambda p0, pn: WINK[l][:, :, cols["GV"] + p0:cols["GV"] + p0 + pn], 1024, KC,
                    lambda kc, blk: HT[:, kc, blk * 128:(blk + 1) * 128], cons)

        try:
          ck("const")
          for l in range(nlayers):
              s16 = S16[l]
              g16 = G16[l]
              s32 = S32[l]
              g32 = G32[l]
              with contextlib.ExitStack() as sL:
                  WA2 = k.sbuf("WA2", [17, 512], F32, sL)
                  k.dma("sp", WA2[:, :], V(w_a2_d.t[l], None))
                  SOWN = k.sbuf("SOWN", [128, 8, 1024], BF16, sL)
                  NEGF = k.sbuf("NEGF", [128, 512], F32, sL)
                  QCB = k.sbuf("QCB", [128, 64], F32, sL)

                  if isB:
                      k.stopped = True
                  for tt in range(2):
                      tsl = slice(tt * TT, (tt + 1) * TT)
                      with contextlib.ExitStack() as sA:
                          HT = k.sbuf("HT", [128, KC, TT], BF16, sA)
                          RAW = k.sbuf("RAW", [128, 4, TT], F32, sA)
                          CKN = k.sbuf("CKN", [128, 4, TT], BF16, sA)
                          KRR = k.sbuf("KRR", [64, TT], F32, sA)
                          KRQ = k.sbuf("KRQ", [64, TT], BF16, sA)
                          TMP = k.sbuf("TMP", [64, TT], F32, sA)
                          KO = [k.sbuf("KO%d" % i, [128, TT], BF16, sA) for i in range(2)]
                          KRO = [k.sbuf("KRO%d" % i, [64, TT], BF16, sA) for i in range(2)]
                          VO = [k.sbuf("VO%d" % i, [128, 1024], BF16, sA) for i in range(2)]
                          rmsnorm_x(l, tt, G_MIX, HT)
                          hT = lambda kc: HT[:, kc, :]

                          def cons_ckv(ci, ps, m):
                              k.copy(RAW[:, ci, :], ps[:, :], eng="act")
                          proj_fm(WINK[l], cols["CKV"], 512, KC, hT, cons_ckv)
                          rl = rvb()
                          rinv_of([(RAW[:, j, :], 128, False) for j in range(4)], 512.0, rl)
                          for j in range(4):
                              k.stt(CKN[:, j, :], RAW[:, j, :], gcol(l, G_CKV + j), rl[:, :], ALU.mult, ALU.mult)
                          ck("A%da" % tt)
                          kr_ps = k.ps()
                          krs_ps = k.ps()
                          Wk = wload(WINK[l][:, :, cols["KR"]:cols["KR"] + 64], KC, 64)
                          for kc in range(KC):
                              k.mm(kr_ps[0:64, :], Wk[:, kc, 0:64], hT(kc), start=(kc == 0), stop=(kc == KC - 1))
                          ck("A%da1" % tt)
                          for kc in range(KC):
                              k.mm(krs_ps[0:32, :], Wk[:, kc, 32:64], hT(kc), start=(kc == 0), stop=(kc == KC - 1))
                          for kc in range(KC):
                              k.mm(krs_ps[32:64, :], Wk[:, kc, 0:32], hT(kc), start=(kc == 0), stop=(kc == KC - 1))
                          ck("A%da2" % tt)
                          k.act(KRQ[:, :], kr_ps[0:64, :], AF.Square)
                          k.stt(KRR[:, :], kr_ps[0:64, :], gcol(l, G_KR, 64), ROPC[:, tsl], ALU.mult, ALU.mult)
                          k.stt(TMP[:, :], krs_ps[0:64, :], gcol(l, G_KRS, 64), ROPS[:, tsl], ALU.mult, ALU.mult)
                          k.tt(KRR[:, :], KRR[:, :], TMP[:, :], ALU.add)
                          ck("A%db" % tt)
                          KA_v = s16.t[R_KA:R_KA + 1536, :].rearrange("(h d) t -> h d t", h=8)
                          for h in range(8):
                              W = wload(WUKV[l][:, :, h * 256:h * 256 + 128], 4, 128)
                              ps = k.ps()
                              for kc in range(4):
                                  k.mm(ps[:, :], W[:, kc, :], CKN[:, kc, :], start=(kc == 0), stop=(kc == 3))
                              rh = rvb()
                              rinv_of([(ps[:, :], 128, False), (KRQ[:, :], 64, True)], 192.0, rh)
                              ko = KO[h % 2]
                              kro = KRO[h % 2]
                              k.stt(ko[:, :], ps[:, :], gcol(l, G_KN), rh[:, :], ALU.mult, ALU.mult)
                              k.tt(kro[:, :], KRR[:, :], rh[0:64, :], ALU.mult)
                              k.dma("sp", V(KA_v[h, 0:128, tsl], s16), ko[:, :])
                              k.dma("sp", V(KA_v[h, 128:192, tsl], s16), kro[:, :])
                          ck("A%dc" % tt)
                          Wv = wload(WUKV[l].rearrange("p k (h two c) -> p k h two c", h=8, two=2)[:, :, :, 1, :], 4, 1024, split=8)
                          VA_v = s16.t[R_VA:R_VA + 1024, :].rearrange("(h t) (j v) -> t j h v", h=8, j=8)
                          for blk in range(4):
                              vo = VO[blk % 2]
                              for half in range(2):
                                  ps = k.ps()
                                  for kc in range(4):
                                      k.mm(ps[:, :], CKN[:, kc, blk * 128:(blk + 1) * 128],
                                           Wv[:, kc, half * 512:(half + 1) * 512], start=(kc == 0), stop=(kc == 3))
                                  k.copy(vo[:, half * 512:(half + 1) * 512], ps[:, :], eng="act" if half else "dve")
                              k.dma("sp", V(VA_v[:, 4 * tt + blk, :, :], s16),
                                    vo.v(vo.t[:, :].rearrange("t (h v) -> t h v", h=8)))
                          ck("A%dd" % tt)
                          KC_v = s16.t[R_KC:R_KC + 1024, :].rearrange("(h d) t -> h d t", h=8)

                          def cons_fk(ci, ps, m):
                              rh = rvb()
                              rinv_of([(ps[:, :], 128, False)], 128.0, rh)
                              ko = KO[ci % 2]
                              k.stt(ko[:, :], ps[:, :], gcol(l, G_FK), rh[:, :], ALU.mult, ALU.mult)
                              k.dma("sp", V(KC_v[ci, :, tsl], s16), ko[:, :])
                          proj_fm(WINK[l], cols["FK"], 1024, KC, hT, cons_fk)
                          ck("A%de" % tt)
                          VC_v = s16.t[R_VC:R_VC + 1024, :].rearrange("(h t) (j v) -> t j h v", h=8, j=8)
                          VB = k.sbuf("VB", [128, 4, 1024], BF16, sA)
                          VBF = VB

                          def cons_fv(blk, p0, pn, ps):
                              k.copy(VBF[:, blk, p0:p0 + pn], ps[:, 0:pn], eng="act" if blk % 2 else "dve")
                          proj_tm(lambda p0, pn: WINK[l][:, :, cols["FV"] + p0:cols["FV"] + p0 + pn], 1024, KC,
                                  lambda kc, blk: HT[:, kc, blk * 128:(blk + 1) * 128], cons_fv)
                          for blk in range(4):
                              k.dma("sp", V(VC_v[:, 4 * tt + blk, :, :], s16),
                                    VBF.v(VBF.t[:, blk, :].rearrange("t (h v) -> t h v", h=8)))
                          ck("A%df" % tt)
                          LFT = k.sbuf("LFT", [128, 4, 8], F32, sA)
                          Wf = wload(WINK[l][:, :, cols["FL"]:cols["FL"] + 8], KC, 8)
                          for blk in range(4):
                              ps = k.ps()
                              for kc in range(KC):
                                  k.mm(ps[:, 0:8], HT[:, kc, blk * 128:(blk + 1) * 128], Wf[:, kc, 0:8],
                                       start=(kc == 0), stop=(kc == KC - 1))
                              k.tt(LFT[:, blk, :], ps[:, 0:8], GN[:, l * GL + G_BF:l * GL + G_BF + 8], ALU.add)
                          k.act(LFT[:, :, :], LFT[:, :, :], AF.Exp, scale=-1.0)
                          k.act(LFT[:, :, :], LFT[:, :, :], AF.Ln, bias=1.0)
                          LF_v = s32.t[R_LF:R_LF + 8, :].rearrange("j (t h) -> t j h", h=8)
                          k.dma("sp", V(LF_v[:, 4 * tt:4 * tt + 4, :], s32), LFT[:, :, :])
                          ck("A%dg" % tt)
                          SPB = k.sbuf("SPB", [128, 4, 512], F32, sA)
                          GAA = k.sbuf("GAA", [17, TT], F32, sA)
                          KTM = k.sbuf("KTM", [128, 4, 512], F32, sA)
                          EN = k.sbuf("EN", [128, 512], F32, sA)
                          KT = k.sbuf("KT", [128, 512], BF16, sA)
                          UB = k.sbuf("UB", [128, 1024], F32, sA)
                          DB = k.sbuf("DB", [128, 4], F32, sA)
                          k.memset(GAA[:, :], 1.0)
                          gla_sp(l, HT, SPB, GAA)
                          gla_v(l, HT, VB)

                          def cons_gk(blk, p0, pn, ps):
                              k.copy(KTM[:, blk, p0:p0 + pn], ps[:, 0:pn], eng="act" if blk % 2 else "dve")
                          proj_tm(lambda p0, pn: WINK[l][:, :, cols["GK"] + p0:cols["GK"] + p0 + pn], 512, KC,
                                  lambda kc, blk: HT[:, kc, blk * 128:(blk + 1) * 128], cons_gk)
                          ck("A%dh" % tt)
                          UX_v = s32.t[R_UX:R_UX + 1024, :].rearrange("(j p) n -> j p n", j=8)
                          DX_v = s32.t[R_DX:R_DX + 4, :].rearrange("a (b f) -> (a b) f", f=4).rearrange(
                              "(j p) f -> j p f", j=8)
                          for blk in range(4):
                              bps = k.ps()
                              k.mm(bps[:, :], TRIS[:, :], SPB[:, blk, :], start=True, stop=True)
                              k.act(EN[:, :], bps[:, :], AF.Exp, scale=-1.0)
                              k.tt(KT[:, :], KTM[:, blk, :], EN[:, :], ALU.mult)
                              dps = k.ps()
                              for h in range(4):
                                  k.mm(dps[:, 2 * h:2 * h + 2], SPB[:, blk, h * 128:(h + 1) * 128], TRIS[:, 126:128],
                                       start=True, stop=True)
                              k.act(DB[:, :], dps.v(dps.t[:, 0:8].rearrange("p (h two) -> p h two", two=2)[:, :, 1]), AF.Exp)
                              for h in range(4):
                                  ups = k.ps()
                                  k.mm(ups[:, 0:256], KT[:, h * 128:(h + 1) * 128], VB[:, blk, h * 256:(h + 1) * 256],
                                       start=True, stop=True)
                                  k.ts(UB[:, h * 256:(h + 1) * 256], ups[:, 0:256], DB[:, h:h + 1], ALU.mult)
                              k.dma("sp", V(UX_v[4 * tt + blk], s32), UB[:, :])
                              k.dma("sp", V(DX_v[4 * tt + blk], s32), DB[:, :])
                          k.barrier()
                          ck("A%d" % tt)

                  if isA:
                      k.stopped = True
                  k.allgather(s32, g32)
                  k.allgather(s16, g16)
                  k.barrier()
                  if isB:
                      k.stopped = False
                  ck("X")

                  with contextlib.ExitStack() as sS:
                      LFA = k.sbuf("LFA", [128, 64, 8], F32, sS)
                      TOTB = k.sbuf("TOTB", [128, 64, 8], F32, sS)
                      INCL = k.sbuf("INCL", [128, 64, 8], F32, sS)
                      ZER = k.sbuf("ZER", [128, 64], F32, sS)
                      OWNT = k.sbuf("OWNT", [128, 8, 8, 8], F32, sS)
                      OWNP = k.sbuf("OWNP", [128, 8, 8], F32, sS)
                      ST = k.sbuf("ST", [128, 1024], F32, sS)
                      UBS = [k.sbuf("UBS%d" % i, [128, 1024], F32, sS) for i in range(2)]
                      DBS = [k.sbuf("DBS%d" % i, [128, 4], F32, sS) for i in range(2)]
                      for r in range(8):
                          src = g32.t[r * R32 + R_LF:r * R32 + R_LF + 8, :].rearrange("j (t h) -> t j h", h=8)
                          dst = LFA.t[:, :, :].rearrange("t (j r) h -> t j r h", r=8)[:, :, r, :]
                          k.dma("sp", LFA.v(dst), V(src, g32))
                      k.memset(ZER[:, :], 0.0)
                      lfa2 = LFA.v(LFA.t[:, :, :].rearrange("t b h -> t (b h)"))
                      fs_ps = k.ps()
                      k.mm(fs_ps[:, :], TRIL1[:, :], lfa2, start=True, stop=True)
                      tot_ps = k.ps()
                      k.mm(tot_ps[:, :], ONESF[:, :], lfa2, start=True, stop=True)
                      k.copy(TOTB.v(TOTB.t[:, :, :].rearrange("t b h -> t (b h)")), tot_ps[:, :])
                      for h in range(8):
                          k.op("dve", lambda E, h=h: E.tensor_tensor_scan(
                              out=INCL.t[:, :, h], data0=TOTB.t[:, :, h], data1=ZER.t[:, :], initial=0.0,
                              op0=ALU.add, op1=ALU.add), [TOTB[:, :, :], ZER[:, :]], [INCL[:, :, :]])
                      k.tt(INCL[:, :, :], INCL[:, :, :], TOTB[:, :, :], ALU.subtract)
                      k.tt(NEGF[:, :], fs_ps[:, :], INCL.v(INCL.t[:, :, :].rearrange("t b h -> t (b h)")), ALU.add)
                      tot4 = TOTB.t[:, :, :].rearrange("t (m r) h -> t m h r", r=8)
                      for r in range(8):
                          k.ts(OWNT[:, :, :, r], TOTB.v(tot4[:, :, :, r]), MASKR(r), ALU.mult)
                      k.op("dve", lambda E: E.tensor_reduce(out=OWNP.t[:, :, :], in_=OWNT.t[:, :, :, :], axis=AX.X,
                                                            op=ALU.add), [OWNT[:, :, :, :]], [OWNP[:, :, :]])
                      ex0 = INCL.t[:, :, :].rearrange("t (m r) h -> t m r h", r=8)[:, :, 0, :]
                      k.tt(OWNP[:, :, :], OWNP[:, :, :], INCL.v(ex0), ALU.add)
                      k.ts(QCB.v(QCB.t[:, :].rearrange("t (m h) -> t m h", h=8)), OWNP[:, :, :], -1.0, ALU.mult)
                      k.memset(ST[:, :], 0.0)
                      DXg = lambda r: g32.t[r * R32 + R_DX:r * R32 + R_DX + 4, :].rearrange(
                          "a (b f) -> (a b) f", f=4).rearrange("(j p) f -> j p f", j=8)
                      for b in range(64):
                          J, r = b // 8, b % 8
                          ub = UBS[b % 2]
                          db = DBS[b % 2]
                          k.dma("sp", ub[:, :], V(g32.t[r * R32 + R_UX + J * 128:r * R32 + R_UX + (J + 1) * 128, :], g32))
                          k.dma("sp", db[:, :], V(DXg(r)[J], g32))
                          if r == 0:
                              k.ts(SOWN[:, J, :], ST[:, :], OH(0), ALU.mult)
                          else:
                              k.stt(SOWN[:, J, :], ST[:, :], OH(r), SOWN[:, J, :], ALU.mult, ALU.add)
                          for h in range(4):
                              hs = slice(h * 256, (h + 1) * 256)
                              k.stt(ST[:, hs], ST[:, hs], db[:, h:h + 1], ub[:, hs], ALU.mult, ALU.add)
                      k.barrier()
                      ck("S")

                  for tt in range(2):
                      tsl = slice(tt * TT, (tt + 1) * TT)
                      nJ = 4 * tt + 4
                      with contextlib.ExitStack() as sB:
                          HT = k.sbuf("HT", [128, KC, TT], BF16, sB)
                          OT = k.sbuf("OT", [128, 24, TT], BF16, sB)
                          rmsnorm_x(l, tt, G_MIX, HT)
                          hT = lambda kc: HT[:, kc, :]
                          with contextlib.ExitStack() as sC:
                              CQN = k.sbuf("CQN", [128, 4, TT], BF16, sC)
                              with contextlib.ExitStack() as sC1:
                                  RAW = k.sbuf("RAW", [128, 4, TT], F32, sC1)

                                  def cons_cq(ci, ps, m):
                                      k.copy(RAW[:, ci, :], ps[:, :], eng="act")
                                  proj_fm(WIN[l], C_CQ, 512, KC, hT, cons_cq)
                                  rl = rvb()
                                  rinv_of([(RAW[:, j, :], 128, False) for j in range(4)], 512.0, rl)
                                  for j in range(4):
                                      k.stt(CQN[:, j, :], RAW[:, j, :], gcol(l, G_CQ + j), rl[:, :], ALU.mult, ALU.mult)
                                  k.barrier()
                              QN = [k.sbuf("QN%d" % i, [128, TT], BF16, sC) for i in range(2)]
                              QR = [k.sbuf("QR%d" % i, [64, TT], BF16, sC) for i in range(2)]
                              T64 = [k.sbuf("T64%d" % i, [64, TT], F32, sC) for i in range(2)]
                              KN = [k.sbuf("KN%d" % i, [128, 1024], BF16, sC) for i in range(2)]
                              KRb = [k.sbuf("KRb%d" % i, [64, 1024], BF16, sC) for i in range(2)]
                              VV = [k.sbuf("VV%d" % i, [128, 1024], BF16, sC) for i in range(2)]
                              PT = [k.sbuf("PT%d" % i, [128, TT], BF16, sC) for i in range(3)]
                              SP32 = [k.sbuf("SP32%d" % i, [128, TT], F32, sC) for i in range(2)]
                              FTB = k.sbuf("FTB", [128, TT], F32, sC)
                              RD = k.sbuf("RD", [128, TT], F32, sC)
                              cnt = dict(kv=0, pt=0, sp=0)

                              def attention(kind, h, Qn, Qr, ot_chunk):
                                  rbase = R_KA if kind == "a" else R_KC
                                  vbase = R_VA if kind == "a" else R_VC
                                  kdim = 192 if kind == "a" else 128
                                  ncol = nJ * 128
                                  first = True
                                  for r in range(8):
                                      i3 = cnt["kv"] % 2
                                      cnt["kv"] += 1
                                      kn, krb, vv = KN[i3], KRb[i3], VV[i3]
                                      kbase = r * R16 + rbase + h * kdim
                                      k.dma("sp", kn[:, 0:ncol], V(g16.t[kbase:kbase + 128, 0:ncol], g16))
                                      if kind == "a":
                                          k.dma("sp", krb[:, 0:ncol], V(g16.t[kbase + 128:kbase + 192, 0:ncol], g16))
                                      vb0 = r * R16 + vbase + h * 128
                                      k.dma("sp", vv[:, 0:ncol], V(g16.t[vb0:vb0 + 128, 0:ncol], g16))
                                      for J in range(nJ):
                                          m0 = max(J, 4 * tt)
                                          c0 = (m0 - 4 * tt) * 128
                                          N = TT - c0
                                          ks = slice(J * 128, (J + 1) * 128)
                                          S = k.ps()
                                          k.mm(S[:, 0:N], kn[:, ks], Qn[:, c0:TT], start=True, stop=(kind != "a"))
                                          if kind == "a":
                                              k.mm(S[:, 0:N], krb[:, ks], Qr[:, c0:TT], start=False, stop=True)
                                          P = PT[cnt["pt"] % 3]
                                          cnt["pt"] += 1
                                          if kind == "c":
                                              sp = SP32[cnt["sp"] % 2]
                                              cnt["sp"] += 1
                                              k.tt(sp[:, 0:N], S[:, 0:N], FTB[:, c0:TT], ALU.add)
                                              bcol = (8 * J + r) * 8 + h
                                              if J >= 4 * tt:
                                                  k.ts(sp[:, 0:N], sp[:, 0:N], NEGF[:, bcol:bcol + 1], ALU.add, 60.0, ALU.min)
                                                  k.act(P[:, 0:N], sp[:, 0:N], AF.Exp)
                                              else:
                                                  k.act(P[:, 0:N], sp[:, 0:N], AF.Exp, bias=NEGF[:, bcol:bcol + 1])
                                          else:
                                              k.act(P[:, 0:N], S[:, 0:N], AF.Exp)
                                          if J >= 4 * tt:
                                              k.tt(P[:, 0:128], P[:, 0:128], MK[:, r, :], ALU.mult)
                                          last = (r == 7 and J == nJ - 1)
                                          k.mm(PSO[:, c0:TT], vv[:, ks], P[:, 0:N], start=first, stop=last)
                                          k.mm(PSD[:, c0:TT], ONESB[:, :], P[:, 0:N], start=first, stop=last)
                                          first = False
                                  k.recip(RD[:, :], PSD[:, :])
                                  k.tt(ot_chunk, PSO[:, :], RD[:, :], ALU.mult)

                              sc_a = 192.0 ** -0.5
                              for h in range(8):
                                  W = wload(WUQ[l][:, :, h * 192:(h + 1) * 192], 4, 192)
                                  qn_ps = k.ps()
                                  qr_ps = k.ps()
                                  qs_ps = k.ps()
                                  for kc in range(4):
                                      k.mm(qn_ps[:, :], W[:, kc, 0:128], CQN[:, kc, :], start=(kc == 0), stop=(kc == 3))
                                  for kc in range(4):
                                      k.mm(qr_ps[0:64, :], W[:, kc, 128:192], CQN[:, kc, :], start=(kc == 0), stop=(kc == 3))
                                  for kc in range(4):
                                      k.mm(qs_ps[0:32, :], W[:, kc, 160:192], CQN[:, kc, :], start=(kc == 0), stop=(kc == 3))
                                  for kc in range(4):
                                      k.mm(qs_ps[32:64, :], W[:, kc, 128:160], CQN[:, kc, :], start=(kc == 0), stop=(kc == 3))
                                  rh = rvb()
                                  rinv_of([(qn_ps[:, :], 128, False), (qr_ps[0:64, :], 64, False)], 192.0, rh)
                                  k.ts(rh[:, :], rh[:, :], sc_a, ALU.mult)
                                  qn, qr = QN[h % 2], QR[h % 2]
                                  t1, t2 = T64
                                  k.stt(qn[:, :], qn_ps[:, :], gcol(l, G_QN), rh[:, :], ALU.mult, ALU.mult)
                                  k.stt(t1[:, :], qr_ps[0:64, :], gcol(l, G_QR, 64), ROPC[:, tsl], ALU.mult, ALU.mult)
                                  k.stt(t2[:, :], qs_ps[0:64, :], gcol(l, G_QRS, 64), ROPS[:, tsl], ALU.mult, ALU.mult)
                                  k.tt(t1[:, :], t1[:, :], t2[:, :], ALU.add)
                                  k.tt(qr[:, :], t1[:, :], rh[0:64, :], ALU.mult)
                                  attention("a", h, qn, qr, OT[:, h, :])
                              sc_c = 128.0 ** -0.5
                              for h in range(8):
                                  W = wload(WIN[l][:, :, C_FQ + h * 128:C_FQ + (h + 1) * 128], KC, 128)
                                  q_ps = k.ps()
                                  for kc in range(KC):
                                      k.mm(q_ps[:, :], W[:, kc, :], hT(kc), start=(kc == 0), stop=(kc == KC - 1))
                                  rh = rvb()
                                  rinv_of([(q_ps[:, :], 128, False)], 128.0, rh)
                                  k.ts(rh[:, :], rh[:, :], sc_c, ALU.mult)
                                  qn = QN[h % 2]
                                  k.stt(qn[:, :], q_ps[:, :], gcol(l, G_FQ), rh[:, :], ALU.mult, ALU.mult)
                                  for mi in range(4):
                                      col = (4 * tt + mi) * 8 + h
                                      k.copy(FTB[:, mi * 128:(mi + 1) * 128], QCB.v(QCB.t[:, col:col + 1].to_broadcast([128, 128])))
                                  attention("c", h, qn, None, OT[:, 16 + h, :])
                              k.barrier()
                              ck("C%d" % tt)
                          with contextlib.ExitStack() as sG:
                              SPB = k.sbuf("SPB", [128, 4, 512], F32, sG)
                              GAA = k.sbuf("GAA", [17, TT], F32, sG)
                              VB = k.sbuf("VB", [128, 4, 1024], BF16, sG)
                              EPH = k.sbuf("EPH", [128, TT], F32, sG)
                              ENH = k.sbuf("ENH", [128, TT], F32, sG)
                              QT = k.sbuf("QT", [128, TT], BF16, sG)
                              KTt = k.sbuf("KTt", [128, TT], BF16, sG)
                              AT = k.sbuf("AT", [128, TT], BF16, sG)
                              OG = k.sbuf("OG", [128, 2, TT], F32, sG)
                              SG = k.sbuf("SG", [128, TT], F32, sG)
                              k.memset(GAA[:, :], 1.0)
                              gla_sp(l, HT, SPB, GAA)
                              gla_v(l, HT, VB)
                              sc_b = 128.0 ** -0.5
                              for h in range(4):
                                  bt = k.ps()
                                  for blk in range(4):
                                      k.mm(bt[:, blk * 128:(blk + 1) * 128], SPB[:, blk, h * 128:(h + 1) * 128], TRIS[:, :],
                                           start=True, stop=True)
                                  k.act(EPH[:, :], bt[:, :], AF.Exp)
                                  k.act(ENH[:, :], bt[:, :], AF.Exp, scale=-1.0)
                                  Wq = wload(WIN[l][:, :, C_GQ + h * 128:C_GQ + (h + 1) * 128], KC, 128)
                                  q_ps = k.ps()
                                  for kc in range(KC):
                                      k.mm(q_ps[:, :], Wq[:, kc, :], hT(kc), start=(kc == 0), stop=(kc == KC - 1))
                                  k.stt(QT[:, :], q_ps[:, :], sc_b, EPH[:, :], ALU.mult, ALU.mult)
                                  Wk2 = wload(WINK[l][:, :, cols["GK"] + h * 128:cols["GK"] + (h + 1) * 128], KC, 128)
                                  k_ps = k.ps()
                                  for kc in range(KC):
                                      k.mm(k_ps[:, :], Wk2[:, kc, :], hT(kc), start=(kc == 0), stop=(kc == KC - 1))
                                  k.tt(KTt[:, :], k_ps[:, :], ENH[:, :], ALU.mult)
                                  a_ps = k.ps()
                                  for blk in range(4):
                                      bs = slice(blk * 128, (blk + 1) * 128)
                                      k.mm(a_ps[:, bs], KTt[:, bs], QT[:, bs], start=True, stop=True)
                                  for blk in range(4):
                                      bs = slice(blk * 128, (blk + 1) * 128)
                                      k.tt(AT[:, bs], a_ps[:, bs], TRI[:, :], ALU.mult)
                                  for half in range(2):
                                      o_ps = k.ps()
                                      vs = slice(h * 256 + half * 128, h * 256 + (half + 1) * 128)
                                      for blk in range(4):
                                          bs = slice(blk * 128, (blk + 1) * 128)
                                          k.mm(o_ps[:, bs], VB[:, blk, vs], AT[:, bs], start=True, stop=False)
                                          k.mm(o_ps[:, bs], SOWN[:, 4 * tt + blk, vs], QT[:, bs], start=False, stop=True)
                                      k.copy(OG[:, half, :], o_ps[:, :], eng="act")
                                  rh = rvb()
                                  rinv_of([(OG[:, 0, :], 128, False), (OG[:, 1, :], 128, False)], 256.0, rh)
                                  Wr = wload(WIN[l][:, :, C_GR + h * 256:C_GR + (h + 1) * 256], KC, 256)
                                  for half in range(2):
                                      g_ps = k.ps()
                                      for kc in range(KC):
                                          k.mm(g_ps[:, :], Wr[:, kc, half * 128:(half + 1) * 128], hT(kc),
                                               start=(kc == 0), stop=(kc == KC - 1))
                                      k.act(SG[:, :], g_ps[:, :], AF.Silu)
                                      k.stt(OG[:, half, :], OG[:, half, :], gcol(l, G_GLA + half), rh[:, :], ALU.mult, ALU.mult)
                                      k.tt(OT[:, 8 + 2 * h + half, :], OG[:, half, :], SG[:, :], ALU.mult)
                              k.barrier()
                              ck("G%d" % tt)
                          with contextlib.ExitStack() as sD:
                              MT = k.sbuf("MT", [128, KC, TT], BF16, sD)
                              SGD = [k.sbuf("SGD%d" % i, [128, TT], F32, sD) for i in range(2)]
                              ACC = [k.sbuf("ACC%d" % i, [128, TT], F32, sD) for i in range(2)]
                              for dp in range(8):
                                  for n in range(3):
                                      Wbn = wload(WBR[l][n][:, :, dp * 256:(dp + 1) * 256], 8, 256)
                                      Wg = wload(WIN[l][:, :, C_GATES + n * D + dp * 256:C_GATES + n * D + (dp + 1) * 256], KC, 256)
                                      for ci in range(2):
                                          dc = dp * 2 + ci
                                          cs = slice(ci * 128, (ci + 1) * 128)
                                          y_ps = k.ps()
                                          for kc in range(8):
                                              k.mm(y_ps[:, :], Wbn[:, kc, cs], OT[:, n * 8 + kc, :], start=(kc == 0), stop=(kc == 7))
                                          g_ps = k.ps()
                                          for kc in range(KC):
                                              k.mm(g_ps[:, :], Wg[:, kc, cs], hT(kc), start=(kc == 0), stop=(kc == KC - 1))
                                          sg = SGD[(n * 2 + ci) % 2]
                                          k.act(sg[:, :], g_ps[:, :], AF.Sigmoid)
                                          acc = ACC[ci]
                                          if n == 0:
                                              k.tt(acc[:, :], y_ps[:, :], sg[:, :], ALU.mult)
                                          else:
                                              k.tt(sg[:, :], y_ps[:, :], sg[:, :], ALU.mult)
                                              if n == 1:
                                                  k.tt(acc[:, :], acc[:, :], sg[:, :], ALU.add)
                                              else:
                                                  k.tt(MT[:, dc, :], acc[:, :], sg[:, :], ALU.add)
                              def cons_out(ci, ps, m):
                                  k.tt(XT[:, ci, tsl], XT[:, ci, tsl], ps[:, :], ALU.add)
                              proj_fm(WOUT[l], 0, D, KC, lambda kc: MT[:, kc, :], cons_out)
                              k.barrier()
                              ck("D%d" % tt)
                      with contextlib.ExitStack() as sF:
                          HT = k.sbuf("HT", [128, KC, TT], BF16, sF)
                          AH = k.sbuf("AH", [128, 44, TT], BF16, sF)
                          SGF = [k.sbuf("SGF%d" % i, [128, TT], F32, sF) for i in range(2)]
                          rmsnorm_x(l, tt, G_FFN, HT)
                          hT = lambda kc: HT[:, kc, :]
                          for hp in range(22):
                              Wg = wload(WGU[l][:, :, hp * 256:(hp + 1) * 256], KC, 256)
                              Wu = wload(WGU[l][:, :, HID + hp * 256:HID + (hp + 1) * 256], KC, 256)
                              for ci in range(2):
                                  hc = hp * 2 + ci
                                  cs = slice(ci * 128, (ci + 1) * 128)
                                  g_ps = k.ps()
                                  for kc in range(KC):
                                      k.mm(g_ps[:, :], Wg[:, kc, cs], hT(kc), start=(kc == 0), stop=(kc == KC - 1))
                                  u_ps = k.ps()
                                  for kc in range(KC):
                                      k.mm(u_ps[:, :], Wu[:, kc, cs], hT(kc), start=(kc == 0), stop=(kc == KC - 1))
                                  sg = SGF[hc % 2]
                                  k.act(sg[:, :], g_ps[:, :], AF.Silu)
                                  k.tt(AH[:, hc, :], sg[:, :], u_ps[:, :], ALU.mult)
                          for dc in range(KC):
                              ps = k.ps()
                              for half in range(2):
                                  W = wload(WDN[l][:, half * 22:(half + 1) * 22, dc * 128:(dc + 1) * 128], 22, 128)
                                  for kc in range(22):
                                      k.mm(ps[:, :], W[:, kc, :], AH[:, half * 22 + kc, :],
                                           start=(half == 0 and kc == 0), stop=(half == 1 and kc == 21))
                              k.tt(XT[:, dc, tsl], XT[:, dc, tsl], ps[:, :], ALU.add)
                          k.barrier()
                          ck("F%d" % tt)
                  k.barrier()

        except _Stop:
            pass
        k.stopped = False

        if isA:
            o16 = Buf(nc.dram_tensor("s16_out", [R16, 1024], BF16, kind="ExternalOutput"), "s16_out")
            o32 = Buf(nc.dram_tensor("s32_out", [R32, 1024], F32, kind="ExternalOutput"), "s32_out")
            o16.multi = True
            o32.multi = True
            k.barrier()
            for src, dst, nr in ((S16[0], o16, R16), (S32[0], o32, R32)):
                for r0 in range(0, nr, 512):
                    r1 = min(nr, r0 + 512)
                    k.dma("sp", dst[r0:r1, :], src[r0:r1, :], sembuf=dst)
        else:
            k.dma("sp", out_d[:, :, :], XT[:, :, :])
        k.barrier()
    return nc


def _host_consts(c):
    s = np.arange(128)[:, None]
    t = np.arange(128)[None, :]
    tri = (s <= t).astype(np.float32)
    mk = np.zeros((128, 8, 128), np.float32)
    for r in range(8):
        if r < c:
            mk[:, r, :] = 1.0
        elif r == c:
            mk[:, r, :] = tri
    cvec = np.zeros((128, 20), np.float32)
    for r in range(8):
        cvec[:, r] = 1.0 if r < c else 0.0
        cvec[:, 8 + r] = 1.0 if r == c else 0.0
    half = 32
    inv = (np.float32(10000.0) ** (-(np.arange(half, dtype=np.float32) / np.float32(half)))).astype(np.float32)
    cvec[0:64, 16] = np.concatenate([inv, inv])
    cvec[0:64, 17] = np.concatenate([-np.ones(32, np.float32), np.ones(32, np.float32)])
    return dict(
        mk=mk.astype(ml_dtypes.bfloat16), tri=tri.astype(ml_dtypes.bfloat16),
        tris=(tri * np.float32(-1.0 / 16.0)).astype(np.float32), tril1=tri.astype(np.float32), cvec=cvec)


def _pack_gains(inp):
    g = np.zeros((128, DEPTH * GL), np.float32)
    for l in range(DEPTH):
        o = l * GL
        g[:, o + G_MIX:o + G_MIX + 16] = inp["g_mix"][l].reshape(16, 128).T
        g[:, o + G_FFN:o + G_FFN + 16] = inp["g_ffn"][l].reshape(16, 128).T
        g[:, o + G_CQ:o + G_CQ + 4] = inp["g_cq"][l].reshape(4, 128).T
        g[:, o + G_CKV:o + G_CKV + 4] = inp["g_ckv"][l].reshape(4, 128).T
        for nm, cn, cr, cs in (("g_mla_q", G_QN, G_QR, G_QRS), ("g_mla_k", G_KN, G_KR, G_KRS)):
            v = inp[nm][l]
            g[:, o + cn] = v[0:128]
            g[0:64, o + cr] = v[128:192]
            g[0:64, o + cs] = np.concatenate([v[160:192], v[128:160]])
        g[:, o + G_GLA:o + G_GLA + 2] = inp["g_gla_o"][l].reshape(2, 128).T
        g[:, o + G_FQ] = inp["g_fox_q"][l]
        g[:, o + G_FK] = inp["g_fox_k"][l]
        g[:, o + G_BF:o + G_BF + 8] = np.broadcast_to(inp["b_f"][l][None, :], (128, 8))
    return g


_KCOLS = np.concatenate([np.arange(512, 1024), np.arange(1024, 1088), np.arange(1600, 2112), np.arange(2112, 3136),
                         np.arange(3136, 3152), np.arange(5200, 6224), np.arange(6224, 7248), np.arange(7248, 7256)])


def _f32(a):
    return np.ascontiguousarray(a, dtype=np.float32)


def _core_common(inp, c, x_c, lsl):
    gains = _pack_gains(inp)
    if lsl.start:
        gains = np.roll(gains, -lsl.start * GL, axis=1)
    pos = np.asarray(inp["positions"])[0].reshape(8, 8, 128)
    w_a2aug = np.concatenate([inp["w_a2"], inp["b_a"][:, None, :]], axis=1)
    m = dict(gains=_f32(gains), w_a2aug=_f32(w_a2aug[lsl]))
    m.update(_host_consts(c))
    m["xT"] = x_c
    m["pos"] = np.ascontiguousarray(np.broadcast_to(pos[:, c, :].reshape(1, TOK), (64, TOK)).astype(np.int32))
    return m


def _weights_A(inp, lsl):
    return dict(w_ink=_f32(inp["w_in"][lsl][:, :, _KCOLS]), w_ukv=_f32(inp["w_ukv"][lsl]))


def _weights_B(inp, lsl):
    return dict(w_in=_f32(inp["w_in"][lsl]), w_uq=_f32(inp["w_uq"][lsl]), w_branch=_f32(inp["w_branch"][lsl]),
                w_out=_f32(inp["w_out"][lsl]), w_gu=_f32(inp["w_gu"][lsl]), w_down=_f32(inp["w_down"][lsl]))


def _run_fused(nc, x_cores, inp):
    lsl = slice(0, DEPTH)
    wa = _weights_A(inp, lsl)
    wb = _weights_B(inp, lsl)
    shared = dict(wb)
    shared["w_ukv"] = wa["w_ukv"]
    in_maps = []
    for c in range(NCORES):
        m = _core_common(inp, c, x_cores[c], lsl)
        m.update(shared)
        in_maps.append(m)
    res = run_bass_kernel_spmd(nc, in_maps, core_ids=list(range(NCORES)))
    return [np.asarray(res.results[c]["outT"]) for c in range(NCORES)]


def _run_layer(ncA, ncB, x_cores, inp, l):
    lsl = slice(l, l + 1)
    wa = _weights_A(inp, lsl)
    maps = []
    for c in range(NCORES):
        m = _core_common(inp, c, x_cores[c], lsl)
        m.update(wa)
        maps.append(m)
    resA = run_bass_kernel_spmd(ncA, maps, core_ids=list(range(NCORES)))
    g16 = np.ascontiguousarray(np.concatenate([np.asarray(resA.results[c]["s16_out"]) for c in range(NCORES)], axis=0))
    g32 = np.ascontiguousarray(np.concatenate([np.asarray(resA.results[c]["s32_out"]) for c in range(NCORES)], axis=0))
    wb = _weights_B(inp, lsl)
    maps = []
    for c in range(NCORES):
        m = _core_common(inp, c, x_cores[c], lsl)
        m.update(wb)
        m["g16_0"] = g16
        m["g32_0"] = g32
        maps.append(m)
    resB = run_bass_kernel_spmd(ncB, maps, core_ids=list(range(NCORES)))
    return [np.asarray(resB.results[c]["outT"]) for c in range(NCORES)]


def _to_cores(x):
    x = x.reshape(8, 8, 128, D)
    out = []
    for c in range(NCORES):
        xs = x[:, c].reshape(TOK, D)
        out.append(np.ascontiguousarray(xs.T.reshape(KC, 128, TOK).transpose(1, 0, 2)))
    return out


def _from_cores(x_cores):
    out = np.empty((8, 8, 128, D), np.float32)
    for c in range(NCORES):
        xs = x_cores[c].transpose(1, 0, 2).reshape(D, TOK).T
        out[:, c] = xs.reshape(8, 128, D)
    return out.reshape(1, 8192, D)


def kernel(**inputs):
    inp = {k_: np.asarray(v) for k_, v in inputs.items()}
    x_cores = _to_cores(inp["x"][0])
    if FUSED:
        nc = build(DEPTH, "fused")
        x_cores = _run_fused(nc, x_cores, inp)
    else:
        ncA = build(1, "A")
        ncB = build(1, "B")
        for l in range(DEPTH):
            x_cores = _run_layer(ncA, ncB, x_cores, inp, l)
    return _from_cores(x_cores)
```

```python
import contextlib
import math
import numpy as np
import ml_dtypes
import concourse.bass as bass
import concourse.mybir as mybir
from concourse.bass_utils import run_bass_kernel_spmd

F32 = mybir.dt.float32
BF16 = mybir.dt.bfloat16
I32 = mybir.dt.int32
AF = mybir.ActivationFunctionType
ALU = mybir.AluOpType
AX = mybir.AxisListType

NCORES = 8
DEPTH = 4
D = 2048
KC = 16
TOK = 1024
TT = 512
DIN = 13400
HID = 5632
EPS = 1e-6
C_CQ, C_CKV, C_KR, C_GQ, C_GK, C_GV, C_GA, C_GR, C_FQ, C_FK, C_FV, C_FL, C_GATES = (
    0, 512, 1024, 1088, 1600, 2112, 3136, 3152, 4176, 5200, 6224, 7248, 7256)
R_KA, R_VA, R_KC, R_VC, R16 = 0, 1536, 2560, 3584, 4608
R_LF, R_UX, R_DX, R32 = 0, 8, 1032, 1152
G_MIX, G_FFN, G_CQ, G_CKV, G_QN, G_QR, G_QRS, G_KN, G_KR, G_KRS, G_GLA, G_FQ, G_FK, G_BF, GL = (
    0, 16, 32, 36, 40, 41, 42, 43, 44, 45, 46, 48, 49, 50, 58)
FUSED = False


class V:
    __slots__ = ("ap", "buf")

    def __init__(self, ap, buf):
        self.ap = ap
        self.buf = buf


class Buf:
    def __init__(self, t, name):
        self.t = t
        self.name = name
        self.w = {}
        self.r = {}
        self.dkey = None
        self.multi = False
        self.psum = False

    def __getitem__(self, idx):
        return V(self.t[idx], self)

    def v(self, ap):
        return V(ap, self)


class WView:
    def __init__(self, buf, ap):
        self.buf = buf
        self.ap3 = ap

    def __getitem__(self, idx):
        return V(self.ap3[idx], self.buf)


class KB:
    def __init__(self, nc, es):
        self.nc = nc
        self.es = es
        self.E = dict(pe=nc.tensor, act=nc.scalar, dve=nc.vector, pool=nc.gpsimd, sp=nc.sync)
        self.sem = {}
        self.cnt = {}
        self.ekey = {}
        self.epoch = -1
        self.known = {e: {} for e in self.E}
        self.nd = 0
        self.nbuf = 0
        self.psums = []
        self.psi = 0
        self.stopped = False
        self.free_dsems = []
        self.new_epoch()

    def new_epoch(self):
        self.epoch += 1
        for e in list(self.E) + ["cc"]:
            key = "%s@%d" % (e, self.epoch)
            self.sem[key] = self.es.enter_context(self.nc.semaphore("s_%s_%d" % (e, self.epoch)))
            self.cnt[key] = 0
            self.ekey[e] = key

    def release(self, buf):
        if buf.dkey is not None:
            self.free_dsems.append(buf.dkey)
            buf.dkey = None

    def sbuf(self, name, shape, dtype, es=None):
        self.nbuf += 1
        t = (es or self.es).enter_context(self.nc.sbuf_tensor("%s_%d" % (name, self.nbuf), shape, dtype))
        b = Buf(t, name)
        if es is not None:
            es.callback(self.release, b)
        return b

    def dram(self, name, shape, dtype):
        t = self.nc.dram_tensor(name, shape, dtype)
        return Buf(t, name)

    def init_psum(self, nrot):
        for i in range(8):
            t = self.es.enter_context(self.nc.psum_tensor("ps%d" % i, [128, 512], F32))
            self.psums.append(Buf(t, "ps%d" % i))
            self.psums[-1].psum = True
        self.nrot = nrot

    def ps(self):
        b = self.psums[self.psi % self.nrot]
        self.psi += 1
        return b

    def _dkey(self, buf):
        if buf.dkey is None:
            if self.free_dsems:
                key = self.free_dsems.pop()
            else:
                self.nd += 1
                key = "d%d" % self.nd
                self.sem[key] = self.es.enter_context(self.nc.semaphore(key))
                self.cnt[key] = 0
            buf.dkey = key
        return buf.dkey

    def _wait(self, eng, key, val):
        if val <= self.known[eng].get(key, 0):
            return
        self.known[eng][key] = val
        self.E[eng].wait_ge(self.sem[key], val)

    def _deps(self, eng, rb, wb, extra=()):
        deps = {}

        def add(tok):
            if tok is None:
                return
            k_, v_ = tok
            if eng == "pe" and k_.startswith("pe@"):
                return
            if deps.get(k_, 0) < v_:
                deps[k_] = v_

        for b in rb:
            for k_, v_ in b.w.items():
                add((k_, v_))
            if b.psum:
                for k_, v_ in b.r.items():
                    if k_ != self.ekey[eng]:
                        add((k_, v_))
        for b in wb:
            if b.multi:
                continue
            for k_, v_ in b.w.items():
                add((k_, v_))
            for k_, v_ in b.r.items():
                add((k_, v_))
        for t in extra:
            add(t)
        for k_, v_ in deps.items():
            self._wait(eng, k_, v_)

    def _commit(self, tok, rb, wb):
        for b in wb:
            if b.multi:
                if b.w.get(tok[0], 0) < tok[1]:
                    b.w[tok[0]] = tok[1]
            else:
                b.w = {tok[0]: tok[1]}
                b.r = {}
        for b in rb:
            if b in wb:
                continue
            if b.r.get(tok[0], 0) < tok[1]:
                b.r[tok[0]] = tok[1]

    @staticmethod
    def _bufs(vs):
        out = []
        for v in vs:
            if v is None or isinstance(v, (int, float)):
                continue
            if v.buf is not None and v.buf not in out:
                out.append(v.buf)
        return out

    def op(self, eng, fn, reads, writes):
        if self.stopped:
            return
        rb = self._bufs(reads)
        wb = self._bufs(writes)
        self._deps(eng, rb, wb)
        inst = fn(self.E[eng])
        ek = self.ekey[eng]
        self.cnt[ek] += 1
        inst.then_inc(self.sem[ek], 1)
        self._commit((ek, self.cnt[ek]), rb, wb)

    def dma(self, q, out, in_, sembuf=None):
        if self.stopped:
            return
        if sembuf is not None:
            sb = sembuf
        elif out.buf is not None and not (out.buf.multi and in_.buf is not None):
            sb = out.buf
        else:
            sb = in_.buf
        key = self._dkey(sb)
        rb = self._bufs([in_])
        wb = self._bufs([out])
        prev = (key, self.cnt[key]) if self.cnt[key] else None
        self._deps(q, rb, wb, extra=(prev,))
        self.E[q].dma_start(out=out.ap, in_=in_.ap).then_inc(self.sem[key], 16)
        self.cnt[key] += 16
        self._commit((key, self.cnt[key]), rb, wb)

    def allgather(self, send, recv):
        if self.stopped:
            return
        rb = [send]
        wb = [recv]
        self._deps("pool", rb, wb)
        self.nc.gpsimd.collective_compute(
            "AllGather", ALU.bypass, replica_groups=[list(range(NCORES))],
            ins=[send.t.ap().opt()], outs=[recv.t.ap().opt()]).then_inc(self.sem[self.ekey["cc"]])
        ck_ = self.ekey["cc"]
        self.cnt[ck_] += 1
        self._commit((ck_, self.cnt[ck_]), rb, wb)

    def barrier(self, engines=None):
        if self.stopped:
            return
        for e in (engines or self.E):
            for key, c in self.cnt.items():
                if c:
                    self._wait(e, key, c)

    def mm(self, out, lhsT, rhs, start, stop):
        self.op("pe", lambda E: E.matmul(out.ap, lhsT=lhsT.ap, rhs=rhs.ap, start=start, stop=stop),
                [lhsT, rhs] + ([] if start else [out]), [out])

    def act(self, out, in_, func, bias=None, scale=1.0):
        kw = {}
        if bias is not None:
            kw["bias"] = bias.ap if isinstance(bias, V) else bias
        self.op("act", lambda E: E.activation(out=out.ap, in_=in_.ap, func=func, scale=scale, **kw),
                [in_, bias], [out])

    def tt(self, out, in0, in1, op, eng="dve"):
        self.op(eng, lambda E: E.tensor_tensor(out=out.ap, in0=in0.ap, in1=in1.ap, op=op), [in0, in1], [out])

    def ts(self, out, in0, s1, op0, s2=None, op1=None, eng="dve"):
        a1 = s1.ap if isinstance(s1, V) else s1
        a2 = s2.ap if isinstance(s2, V) else s2
        if op1 is None:
            fn = lambda E: E.tensor_scalar(out=out.ap, in0=in0.ap, scalar1=a1, scalar2=None, op0=op0)
        else:
            fn = lambda E: E.tensor_scalar(out=out.ap, in0=in0.ap, scalar1=a1, scalar2=a2, op0=op0, op1=op1)
        self.op(eng, fn, [in0, s1, s2], [out])

    def stt(self, out, in0, scalar, in1, op0, op1):
        a = scalar.ap if isinstance(scalar, V) else scalar
        self.op("dve", lambda E: E.scalar_tensor_tensor(out=out.ap, in0=in0.ap, scalar=a, in1=in1.ap,
                                                        op0=op0, op1=op1), [in0, scalar, in1], [out])

    def copy(self, out, in_, eng="dve"):
        if eng == "act":
            self.op(eng, lambda E: E.activation(out=out.ap, in_=in_.ap, func=AF.Copy), [in_], [out])
        else:
            self.op(eng, lambda E: E.tensor_copy(out=out.ap, in_=in_.ap), [in_], [out])

    def memset(self, out, val, eng="dve"):
        self.op(eng, lambda E: E.memset(out.ap, val), [], [out])

    def recip(self, out, in_):
        self.op("dve", lambda E: E.reciprocal(out=out.ap, in_=in_.ap), [in_], [out])


class _Stop(Exception):
    pass


class _Dummy:
    def __getitem__(self, idx):
        return self

    def rearrange(self, *a, **kw):
        return self

    def ap(self):
        return self

    def opt(self):
        return self


def build(nlayers, mode="fused"):
    import os
    stop_at = os.environ.get("K_STOP", "")

    kref = []

    def ck(name):
        if stop_at == name and not kref[0].stopped:
            kref[0].barrier()
            kref[0].stopped = True
    nc = bass.Bass("TRN2", target_bir_lowering=False)
    es = contextlib.ExitStack()
    with es:
        k = KB(nc, es)
        kref.append(k)

        def ext_in(name, shape, dt):
            return Buf(nc.dram_tensor(name, shape, dt, kind="ExternalInput"), name)

        xT_d = ext_in("xT", [128, KC, TOK], F32)
        pos_d = ext_in("pos", [64, TOK], I32)
        gains_d = ext_in("gains", [128, DEPTH * GL], F32)
        mk_d = ext_in("mk", [128, 8, 128], BF16)
        tri_d = ext_in("tri", [128, 128], BF16)
        tris_d = ext_in("tris", [128, 128], F32)
        tril1_d = ext_in("tril1", [128, 128], F32)
        cvec_d = ext_in("cvec", [128, 20], F32)
        def wdecl(name, shape, used):
            if used:
                return ext_in(name, shape, F32)
            return Buf(_Dummy(), name)
        isA, isB, isF = mode == "A", mode == "B", mode == "fused"
        if isA:
            cols = dict(CKV=0, KR=512, GK=576, GV=1088, GA=2112, FK=2128, FV=3152, FL=4176)
            w_ink_d = ext_in("w_ink", [nlayers, D, 4184], F32)
        else:
            cols = dict(CKV=C_CKV, KR=C_KR, GK=C_GK, GV=C_GV, GA=C_GA, FK=C_FK, FV=C_FV, FL=C_FL)
        w_in_d = wdecl("w_in", [nlayers, D, DIN], not isA)
        if not isA:
            w_ink_d = w_in_d
        w_uq_d = wdecl("w_uq", [nlayers, 512, 1536], not isA)
        w_ukv_d = wdecl("w_ukv", [nlayers, 512, 2048], not isB)
        w_a2_d = ext_in("w_a2aug", [nlayers, 17, 512], F32)
        w_br_d = wdecl("w_branch", [nlayers, 3, 1024, D], not isA)
        w_out_d = wdecl("w_out", [nlayers, D, D], not isA)
        w_gu_d = wdecl("w_gu", [nlayers, D, 2 * HID], not isA)
        w_dn_d = wdecl("w_down", [nlayers, HID, D], not isA)
        if not isA:
            out_d = Buf(nc.dram_tensor("outT", [128, KC, TOK], F32, kind="ExternalOutput"), "outT")
        for b in (xT_d, pos_d, gains_d, mk_d, tri_d, tris_d, tril1_d, cvec_d):
            pass
        WIN = [w_in_d.t[l].rearrange("(kc p) n -> p kc n", p=128) for l in range(nlayers)]
        WINK = [w_ink_d.t[l].rearrange("(kc p) n -> p kc n", p=128) for l in range(nlayers)]
        WUQ = [w_uq_d.t[l].rearrange("(kc p) n -> p kc n", p=128) for l in range(nlayers)]
        WUKV = [w_ukv_d.t[l].rearrange("(kc p) n -> p kc n", p=128) for l in range(nlayers)]
        WBR = [[w_br_d.t[l, n].rearrange("(kc p) n -> p kc n", p=128) for n in range(3)] for l in range(nlayers)]
        WOUT = [w_out_d.t[l].rearrange("(kc p) n -> p kc n", p=128) for l in range(nlayers)]
        WGU = [w_gu_d.t[l].rearrange("(kc p) n -> p kc n", p=128) for l in range(nlayers)]
        WDN = [w_dn_d.t[l].rearrange("(kc p) n -> p kc n", p=128) for l in range(nlayers)]

        def xbuf(name, shape, dt, kind):
            if kind == "none":
                return Buf(_Dummy(), name)
            if kind is None:
                return k.dram(name, shape, dt)
            return Buf(nc.dram_tensor(name, shape, dt, kind=kind), name)
        sk = "none" if isB else None
        gk_ = "ExternalInput" if isB else ("none" if isA else None)
        nset = min(nlayers, 2)
        S16 = [xbuf("s16_%d" % l, [R16, 1024], BF16, sk) for l in range(nset)]
        G16 = [xbuf("g16_%d" % l, [8 * R16, 1024], BF16, gk_) for l in range(nset)]
        S32 = [xbuf("s32_%d" % l, [R32, 1024], F32, sk) for l in range(nset)]
        G32 = [xbuf("g32_%d" % l, [8 * R32, 1024], F32, gk_) for l in range(nset)]
        for b in S16 + S32:
            b.multi = True

        k.init_psum(6)
        PSO = k.psums[6]
        PSD = k.psums[7]

        XT = k.sbuf("XT", [128, KC, TOK], F32)
        GN = k.sbuf("GN", [128, DEPTH * GL], F32)
        MK = k.sbuf("MK", [128, 8, 128], BF16)
        TRI = k.sbuf("TRI", [128, 128], BF16)
        TRIS = k.sbuf("TRIS", [128, 128], F32)
        TRIL1 = k.sbuf("TRIL1", [128, 128], F32)
        CVEC = k.sbuf("CVEC", [128, 20], F32)
        ONESB = k.sbuf("ONESB", [128, 128], BF16)
        ONESF = k.sbuf("ONESF", [128, 128], F32)
        ROPC = k.sbuf("ROPC", [64, TOK], F32)
        ROPS = k.sbuf("ROPS", [64, TOK], F32)
        WR = [k.sbuf("WR%d" % i, [128, 4096], BF16) for i in range(3)]
        SQ = [k.sbuf("SQ%d" % i, [128, TT], BF16) for i in range(2)]
        RV = [k.sbuf("RV%d" % i, [128, TT], F32) for i in range(3)]
        state = dict(wi=0, sq=0, rv=0)

        def sqb():
            state["sq"] += 1
            return SQ[state["sq"] % 2]

        def rvb():
            state["rv"] += 1
            return RV[state["rv"] % 3]

        def wload(src_ap, kcn, n, split=None):
            slot = WR[state["wi"] % 3]
            state["wi"] += 1
            v3 = slot.t[:, 0:kcn * n].rearrange("p (k n) -> p k n", k=kcn)
            if split is None:
                k.dma("pool", V(v3, slot), V(src_ap, None))
            else:
                dst = slot.t[:, 0:kcn * n].rearrange("p (k h c) -> p k h c", k=kcn, h=split)
                for kc in range(kcn):
                    k.dma("pool", V(dst[:, kc], slot), V(src_ap[:, kc], None))
            return WView(slot, v3)

        k.dma("sp", XT[:, :, :], xT_d[:, :, :])
        k.dma("sp", GN[:, :], gains_d[:, :])
        k.dma("sp", MK[:, :, :], mk_d[:, :, :])
        k.dma("sp", TRI[:, :], tri_d[:, :])
        k.dma("sp", TRIS[:, :], tris_d[:, :])
        k.dma("sp", TRIL1[:, :], tril1_d[:, :])
        k.dma("sp", CVEC[:, :], cvec_d[:, :])
        k.memset(ONESB[:, :], 1.0)
        k.memset(ONESF[:, :], 1.0)
        MASKR = lambda r: CVEC[:, r:r + 1]
        OH = lambda r: CVEC[:, 8 + r:9 + r]

        with contextlib.ExitStack() as s0:
            POSI = k.sbuf("POSI", [64, TOK], I32, s0)
            ANG = k.sbuf("ANG", [64, TOK], F32, s0)
            T1 = k.sbuf("T1", [64, TOK], F32, s0)
            T2 = k.sbuf("T2", [64, TOK], F32, s0)
            k.dma("sp", POSI[:, :], pos_d[:, :])
            k.copy(ANG[:, :], POSI[:, :])
            k.ts(ANG[:, :], ANG[:, :], CVEC[0:64, 16:17], ALU.mult)
            MAGIC = 12582912.0
            C1 = 6.28125
            C2 = 2.0 * math.pi - 6.28125
            for which, dst in ((0, ROPS), (1, ROPC)):
                k.ts(T1[:, :], ANG[:, :], 1.0 / (2.0 * math.pi), ALU.mult, (0.25 if which else 0.0), ALU.add)
                k.ts(T1[:, :], T1[:, :], MAGIC, ALU.add)
                k.ts(T1[:, :], T1[:, :], -MAGIC, ALU.add)
                k.stt(T2[:, :], T1[:, :], -C1, ANG[:, :], ALU.mult, ALU.add)
                k.stt(T2[:, :], T1[:, :], -C2, T2[:, :], ALU.mult, ALU.add)
                if which:
                    k.ts(T2[:, :], T2[:, :], math.pi / 2.0, ALU.add)
                k.ts(T2[:, :], T2[:, :], 3.1415925, ALU.min, -3.1415925, ALU.max)
                k.act(dst[:, :], T2[:, :], AF.Sin)
            k.ts(ROPS[:, :], ROPS[:, :], CVEC[0:64, 17:18], ALU.mult)
            k.barrier()

        def rinv_of(parts, nfeat, out):
            ss = k.ps()
            n = len(parts)
            for i, (p, rows, pre) in enumerate(parts):
                if pre:
                    s = p
                else:
                    sq = sqb()
                    k.act(sq[0:rows, :], p, AF.Square)
                    s = sq[0:rows, :]
                k.mm(ss[:, :], ONESB[0:rows, :], s, start=(i == 0), stop=(i == n - 1))
            k.act(out[:, :], ss[:, :], AF.Ln, bias=EPS, scale=1.0 / nfeat)
            k.act(out[:, :], out[:, :], AF.Exp, scale=-0.5)

        def rmsnorm_x(l, tt, gcol, HT, off=0):
            tsl = slice(tt * TT, (tt + 1) * TT)
            ss = k.ps()
            for kc in range(KC):
                sq = sqb()
                k.act(sq[:, :], XT[:, kc, tsl], AF.Square)
                k.mm(ss[:, :], ONESB[:, :], sq[:, :], start=(kc == 0), stop=(kc == KC - 1))
            rs = rvb()
            k.act(rs[:, :], ss[:, :], AF.Ln, bias=EPS, scale=1.0 / D)
            k.act(rs[:, :], rs[:, :], AF.Exp, scale=-0.5)
            for kc in range(KC):
                k.stt(HT[:, kc, off:off + TT], XT[:, kc, tsl], GN[:, l * GL + gcol + kc:l * GL + gcol + kc + 1], rs[:, :],
                      ALU.mult, ALU.mult)

        def proj_fm(wsrc, col0, ncols, kcn, rhs_fn, consume):
            for p0 in range(0, ncols, 256):
                pn = min(256, ncols - p0)
                W = wload(wsrc[:, :, col0 + p0:col0 + p0 + pn], kcn, pn)
                for c in range(0, pn, 128):
                    m = min(128, pn - c)
                    ps = k.ps()
                    for kc in range(kcn):
                        k.mm(ps[0:m, :], W[:, kc, c:c + m], rhs_fn(kc), start=(kc == 0), stop=(kc == kcn - 1))
                    consume((p0 + c) // 128, ps, m)

        def proj_tm(wsrc_fn, ncols, kcn, lhs_fn, consume):
            for p0 in range(0, ncols, 256):
                pn = min(256, ncols - p0)
                W = wload(wsrc_fn(p0, pn), kcn, pn)
                for blk in range(4):
                    ps = k.ps()
                    for kc in range(kcn):
                        k.mm(ps[:, 0:pn], lhs_fn(kc, blk), W[:, kc, 0:pn], start=(kc == 0), stop=(kc == kcn - 1))
                    consume(blk, p0, pn, ps)

        def gcol(l, c, rows=128):
            return GN[0:rows, l * GL + c:l * GL + c + 1]

        def gla_sp(l, HT, SPB, GAA):
            def cons(ci, ps, m):
                k.copy(GAA[0:16, :], ps[0:16, :])
            proj_fm(WINK[l], cols["GA"], 16, KC, lambda kc: HT[:, kc, :], cons)
            for blk in range(4):
                z = k.ps()
                k.mm(z[:, :], GAA[0:17, blk * 128:(blk + 1) * 128], WA2[0:17, :], start=True, stop=True)
                k.act(SPB[:, blk, :], z[:, :], AF.Exp, scale=-1.0)
                k.act(SPB[:, blk, :], SPB[:, blk, :], AF.Ln, bias=1.0)

        def gla_v(l, HT, VB):
            def cons(blk, p0, pn, ps):
                k.copy(VB[:, blk, p0:p0 + pn], ps[:, 0:pn], eng="act" if blk % 2 else "dve")
            proj_tm(lambda p0, pn: WINK[l][:, :, cols["GV"] + p0:cols["GV"] + p0 + pn], 1024, KC,
                    lambda kc, blk: HT[:, kc, blk * 128:(blk + 1) * 128], cons)

        try:
          ck("const")
          for l in range(nlayers):
              s16 = S16[l % 2]
              g16 = G16[l % 2]
              s32 = S32[l % 2]
              g32 = G32[l % 2]
              if l > 0:
                  for b_ in (s16, s32):
                      b_.w = {}
                      b_.r = {}
                  k.new_epoch()
              with contextlib.ExitStack() as sL:
                  WA2 = k.sbuf("WA2", [17, 512], F32, sL)
                  k.dma("sp", WA2[:, :], V(w_a2_d.t[l], None))

                  if isB:
                      k.stopped = True
                  for tt in range(2):
                      tsl = slice(tt * TT, (tt + 1) * TT)
                      with contextlib.ExitStack() as sA:
                          HT = k.sbuf("HT", [128, KC, TT], BF16, sA)
                          RAW = k.sbuf("RAW", [128, 4, TT], F32, sA)
                          CKN = k.sbuf("CKN", [128, 4, TT], BF16, sA)
                          KRR = k.sbuf("KRR", [64, TT], F32, sA)
                          KRQ = k.sbuf("KRQ", [64, TT], BF16, sA)
                          TMP = k.sbuf("TMP", [64, TT], F32, sA)
                          KO = [k.sbuf("KO%d" % i, [128, TT], BF16, sA) for i in range(2)]
                          KRO = [k.sbuf("KRO%d" % i, [64, TT], BF16, sA) for i in range(2)]
                          VO = [k.sbuf("VO%d" % i, [128, 1024], BF16, sA) for i in range(2)]
                          rmsnorm_x(l, tt, G_MIX, HT)
                          hT = lambda kc: HT[:, kc, :]

                          def cons_ckv(ci, ps, m):
                              k.copy(RAW[:, ci, :], ps[:, :], eng="act")
                          proj_fm(WINK[l], cols["CKV"], 512, KC, hT, cons_ckv)
                          rl = rvb()
                          rinv_of([(RAW[:, j, :], 128, False) for j in range(4)], 512.0, rl)
                          for j in range(4):
                              k.stt(CKN[:, j, :], RAW[:, j, :], gcol(l, G_CKV + j), rl[:, :], ALU.mult, ALU.mult)
                          ck("A%da" % tt)
                          kr_ps = k.ps()
                          krs_ps = k.ps()
                          Wk = wload(WINK[l][:, :, cols["KR"]:cols["KR"] + 64], KC, 64)
                          for kc in range(KC):
                              k.mm(kr_ps[0:64, :], Wk[:, kc, 0:64], hT(kc), start=(kc == 0), stop=(kc == KC - 1))
                          ck("A%da1" % tt)
                          for kc in range(KC):
                              k.mm(krs_ps[0:32, :], Wk[:, kc, 32:64], hT(kc), start=(kc == 0), stop=(kc == KC - 1))
                          for kc in range(KC):
                              k.mm(krs_ps[32:64, :], Wk[:, kc, 0:32], hT(kc), start=(kc == 0), stop=(kc == KC - 1))
                          ck("A%da2" % tt)
                          k.act(KRQ[:, :], kr_ps[0:64, :], AF.Square)
                          k.stt(KRR[:, :], kr_ps[0:64, :], gcol(l, G_KR, 64), ROPC[:, tsl], ALU.mult, ALU.mult)
                          k.stt(TMP[:, :], krs_ps[0:64, :], gcol(l, G_KRS, 64), ROPS[:, tsl], ALU.mult, ALU.mult)
                          k.tt(KRR[:, :], KRR[:, :], TMP[:, :], ALU.add)
                          ck("A%db" % tt)
                          KA_v = s16.t[R_KA:R_KA + 1536, :].rearrange("(h d) t -> h d t", h=8)
                          for h in range(8):
                              W = wload(WUKV[l][:, :, h * 256:h * 256 + 128], 4, 128)
                              ps = k.ps()
                              for kc in range(4):
                                  k.mm(ps[:, :], W[:, kc, :], CKN[:, kc, :], start=(kc == 0), stop=(kc == 3))
                              rh = rvb()
                              rinv_of([(ps[:, :], 128, False), (KRQ[:, :], 64, True)], 192.0, rh)
                              ko = KO[h % 2]
                              kro = KRO[h % 2]
                              k.stt(ko[:, :], ps[:, :], gcol(l, G_KN), rh[:, :], ALU.mult, ALU.mult)
                              k.tt(kro[:, :], KRR[:, :], rh[0:64, :], ALU.mult)
                              k.dma("sp", V(KA_v[h, 0:128, tsl], s16), ko[:, :])
                              k.dma("sp", V(KA_v[h, 128:192, tsl], s16), kro[:, :])
                          ck("A%dc" % tt)
                          Wv = wload(WUKV[l].rearrange("p k (h two c) -> p k h two c", h=8, two=2)[:, :, :, 1, :], 4, 1024, split=8)
                          VA_v = s16.t[R_VA:R_VA + 1024, :].rearrange("(h t) (j v) -> t j h v", h=8, j=8)
                          for blk in range(4):
                              vo = VO[blk % 2]
                              for half in range(2):
                                  ps = k.ps()
                                  for kc in range(4):
                                      k.mm(ps[:, :], CKN[:, kc, blk * 128:(blk + 1) * 128],
                                           Wv[:, kc, half * 512:(half + 1) * 512], start=(kc == 0), stop=(kc == 3))
                                  k.copy(vo[:, half * 512:(half + 1) * 512], ps[:, :], eng="act" if half else "dve")
                              k.dma("sp", V(VA_v[:, 4 * tt + blk, :, :], s16),
                                    vo.v(vo.t[:, :].rearrange("t (h v) -> t h v", h=8)))
                          ck("A%dd" % tt)
                          KC_v = s16.t[R_KC:R_KC + 1024, :].rearrange("(h d) t -> h d t", h=8)

                          def cons_fk(ci, ps, m):
                              rh = rvb()
                              rinv_of([(ps[:, :], 128, False)], 128.0, rh)
                              ko = KO[ci % 2]
                              k.stt(ko[:, :], ps[:, :], gcol(l, G_FK), rh[:, :], ALU.mult, ALU.mult)
                              k.dma("sp", V(KC_v[ci, :, tsl], s16), ko[:, :])
                          proj_fm(WINK[l], cols["FK"], 1024, KC, hT, cons_fk)
                          ck("A%de" % tt)
                          VC_v = s16.t[R_VC:R_VC + 1024, :].rearrange("(h t) (j v) -> t j h v", h=8, j=8)
                          VB = k.sbuf("VB", [128, 4, 1024], BF16, sA)
                          VBF = VB

                          def cons_fv(blk, p0, pn, ps):
                              k.copy(VBF[:, blk, p0:p0 + pn], ps[:, 0:pn], eng="act" if blk % 2 else "dve")
                          proj_tm(lambda p0, pn: WINK[l][:, :, cols["FV"] + p0:cols["FV"] + p0 + pn], 1024, KC,
                                  lambda kc, blk: HT[:, kc, blk * 128:(blk + 1) * 128], cons_fv)
                          for blk in range(4):
                              k.dma("sp", V(VC_v[:, 4 * tt + blk, :, :], s16),
                                    VBF.v(VBF.t[:, blk, :].rearrange("t (h v) -> t h v", h=8)))
                          ck("A%df" % tt)
                          LFT = k.sbuf("LFT", [128, 4, 8], F32, sA)
                          Wf = wload(WINK[l][:, :, cols["FL"]:cols["FL"] + 8], KC, 8)
                          for blk in range(4):
                              ps = k.ps()
                              for kc in range(KC):
                                  k.mm(ps[:, 0:8], HT[:, kc, blk * 128:(blk + 1) * 128], Wf[:, kc, 0:8],
                                       start=(kc == 0), stop=(kc == KC - 1))
                              k.tt(LFT[:, blk, :], ps[:, 0:8], GN[:, l * GL + G_BF:l * GL + G_BF + 8], ALU.add)
                          k.act(LFT[:, :, :], LFT[:, :, :], AF.Exp, scale=-1.0)
                          k.act(LFT[:, :, :], LFT[:, :, :], AF.Ln, bias=1.0)
                          LF_v = s32.t[R_LF:R_LF + 8, :].rearrange("j (t h) -> t j h", h=8)
                          k.dma("sp", V(LF_v[:, 4 * tt:4 * tt + 4, :], s32), LFT[:, :, :])
                          ck("A%dg" % tt)
                          SPB = k.sbuf("SPB", [128, 4, 512], F32, sA)
                          GAA = k.sbuf("GAA", [17, TT], F32, sA)
                          KTM = k.sbuf("KTM", [128, 4, 512], F32, sA)
                          EN = k.sbuf("EN", [128, 512], F32, sA)
                          KT = k.sbuf("KT", [128, 512], BF16, sA)
                          UB = k.sbuf("UB", [128, 1024], F32, sA)
                          DB = k.sbuf("DB", [128, 4], F32, sA)
                          k.memset(GAA[:, :], 1.0)
                          gla_sp(l, HT, SPB, GAA)
                          gla_v(l, HT, VB)

                          def cons_gk(blk, p0, pn, ps):
                              k.copy(KTM[:, blk, p0:p0 + pn], ps[:, 0:pn], eng="act" if blk % 2 else "dve")
                          proj_tm(lambda p0, pn: WINK[l][:, :, cols["GK"] + p0:cols["GK"] + p0 + pn], 512, KC,
                                  lambda kc, blk: HT[:, kc, blk * 128:(blk + 1) * 128], cons_gk)
                          ck("A%dh" % tt)
                          UX_v = s32.t[R_UX:R_UX + 1024, :].rearrange("(j p) n -> j p n", j=8)
                          DX_v = s32.t[R_DX:R_DX + 4, :].rearrange("a (b f) -> (a b) f", f=4).rearrange(
                              "(j p) f -> j p f", j=8)
                          for blk in range(4):
                              bps = k.ps()
                              k.mm(bps[:, :], TRIS[:, :], SPB[:, blk, :], start=True, stop=True)
                              k.act(EN[:, :], bps[:, :], AF.Exp, scale=-1.0)
                              k.tt(KT[:, :], KTM[:, blk, :], EN[:, :], ALU.mult)
                              dps = k.ps()
                              for h in range(4):
                                  k.mm(dps[:, 2 * h:2 * h + 2], SPB[:, blk, h * 128:(h + 1) * 128], TRIS[:, 126:128],
                                       start=True, stop=True)
                              k.act(DB[:, :], dps.v(dps.t[:, 0:8].rearrange("p (h two) -> p h two", two=2)[:, :, 1]), AF.Exp)
                              for h in range(4):
                                  ups = k.ps()
                                  k.mm(ups[:, 0:256], KT[:, h * 128:(h + 1) * 128], VB[:, blk, h * 256:(h + 1) * 256],
                                       start=True, stop=True)
                                  k.ts(UB[:, h * 256:(h + 1) * 256], ups[:, 0:256], DB[:, h:h + 1], ALU.mult)
                              k.dma("sp", V(UX_v[4 * tt + blk], s32), UB[:, :])
                              k.dma("sp", V(DX_v[4 * tt + blk], s32), DB[:, :])
                          k.barrier()
                          ck("A%d" % tt)

                  if isA:
                      k.stopped = True
                  k.allgather(s32, g32)
                  k.allgather(s16, g16)
                  k.barrier()
                  if isB:
                      k.stopped = False
                  ck("X")

                  sM = contextlib.ExitStack()
                  SOWN = k.sbuf("SOWN", [128, 8, 1024], BF16, sM)
                  NEGF = k.sbuf("NEGF", [128, 512], F32, sM)
                  QCB = k.sbuf("QCB", [128, 64], F32, sM)
                  with contextlib.ExitStack() as sS:
                      LFA = k.sbuf("LFA", [128, 64, 8], F32, sS)
                      TOTB = k.sbuf("TOTB", [128, 64, 8], F32, sS)
                      INCL = k.sbuf("INCL", [128, 64, 8], F32, sS)
                      ZER = k.sbuf("ZER", [128, 64], F32, sS)
                      OWNT = k.sbuf("OWNT", [128, 8, 8, 8], F32, sS)
                      OWNP = k.sbuf("OWNP", [128, 8, 8], F32, sS)
                      ST = k.sbuf("ST", [128, 1024], F32, sS)
                      UBS = [k.sbuf("UBS%d" % i, [128, 1024], F32, sS) for i in range(2)]
                      DBS = [k.sbuf("DBS%d" % i, [128, 4], F32, sS) for i in range(2)]
                      for r in range(8):
                          src = g32.t[r * R32 + R_LF:r * R32 + R_LF + 8, :].rearrange("j (t h) -> t j h", h=8)
                          dst = LFA.t[:, :, :].rearrange("t (j r) h -> t j r h", r=8)[:, :, r, :]
                          k.dma("sp", LFA.v(dst), V(src, g32))
                      k.memset(ZER[:, :], 0.0)
                      lfa2 = LFA.v(LFA.t[:, :, :].rearrange("t b h -> t (b h)"))
                      fs_ps = k.ps()
                      k.mm(fs_ps[:, :], TRIL1[:, :], lfa2, start=True, stop=True)
                      tot_ps = k.ps()
                      k.mm(tot_ps[:, :], ONESF[:, :], lfa2, start=True, stop=True)
                      k.copy(TOTB.v(TOTB.t[:, :, :].rearrange("t b h -> t (b h)")), tot_ps[:, :])
                      for h in range(8):
                          k.op("dve", lambda E, h=h: E.tensor_tensor_scan(
                              out=INCL.t[:, :, h], data0=TOTB.t[:, :, h], data1=ZER.t[:, :], initial=0.0,
                              op0=ALU.add, op1=ALU.add), [TOTB[:, :, :], ZER[:, :]], [INCL[:, :, :]])
                      k.tt(INCL[:, :, :], INCL[:, :, :], TOTB[:, :, :], ALU.subtract)
                      k.tt(NEGF[:, :], fs_ps[:, :], INCL.v(INCL.t[:, :, :].rearrange("t b h -> t (b h)")), ALU.add)
                      tot4 = TOTB.t[:, :, :].rearrange("t (m r) h -> t m h r", r=8)
                      for r in range(8):
                          k.ts(OWNT[:, :, :, r], TOTB.v(tot4[:, :, :, r]), MASKR(r), ALU.mult)
                      k.op("dve", lambda E: E.tensor_reduce(out=OWNP.t[:, :, :], in_=OWNT.t[:, :, :, :], axis=AX.X,
                                                            op=ALU.add), [OWNT[:, :, :, :]], [OWNP[:, :, :]])
                      ex0 = INCL.t[:, :, :].rearrange("t (m r) h -> t m r h", r=8)[:, :, 0, :]
                      k.tt(OWNP[:, :, :], OWNP[:, :, :], INCL.v(ex0), ALU.add)
                      k.ts(QCB.v(QCB.t[:, :].rearrange("t (m h) -> t m h", h=8)), OWNP[:, :, :], -1.0, ALU.mult)
                      k.memset(ST[:, :], 0.0)
                      DXg = lambda r: g32.t[r * R32 + R_DX:r * R32 + R_DX + 4, :].rearrange(
                          "a (b f) -> (a b) f", f=4).rearrange("(j p) f -> j p f", j=8)
                      for b in range(64):
                          J, r = b // 8, b % 8
                          ub = UBS[b % 2]
                          db = DBS[b % 2]
                          k.dma("sp", ub[:, :], V(g32.t[r * R32 + R_UX + J * 128:r * R32 + R_UX + (J + 1) * 128, :], g32))
                          k.dma("sp", db[:, :], V(DXg(r)[J], g32))
                          if r == 0:
                              k.ts(SOWN[:, J, :], ST[:, :], OH(0), ALU.mult)
                          else:
                              k.stt(SOWN[:, J, :], ST[:, :], OH(r), SOWN[:, J, :], ALU.mult, ALU.add)
                          for h in range(4):
                              hs = slice(h * 256, (h + 1) * 256)
                              k.stt(ST[:, hs], ST[:, hs], db[:, h:h + 1], ub[:, hs], ALU.mult, ALU.add)
                      k.barrier()
                      ck("S")

                  for tt in range(2):
                      tsl = slice(tt * TT, (tt + 1) * TT)
                      nJ = 4 * tt + 4
                      with contextlib.ExitStack() as sB:
                          HT = k.sbuf("HT", [128, KC, TT], BF16, sB)
                          OT = k.sbuf("OT", [128, 24, TT], BF16, sB)
                          rmsnorm_x(l, tt, G_MIX, HT)
                          hT = lambda kc: HT[:, kc, :]
                          with contextlib.ExitStack() as sC:
                              CQN = k.sbuf("CQN", [128, 4, TT], BF16, sC)
                              with contextlib.ExitStack() as sC1:
                                  RAW = k.sbuf("RAW", [128, 4, TT], F32, sC1)

                                  def cons_cq(ci, ps, m):
                                      k.copy(RAW[:, ci, :], ps[:, :], eng="act")
                                  proj_fm(WIN[l], C_CQ, 512, KC, hT, cons_cq)
                                  rl = rvb()
                                  rinv_of([(RAW[:, j, :], 128, False) for j in range(4)], 512.0, rl)
                                  for j in range(4):
                                      k.stt(CQN[:, j, :], RAW[:, j, :], gcol(l, G_CQ + j), rl[:, :], ALU.mult, ALU.mult)
                                  k.barrier()
                              QN = [k.sbuf("QN%d" % i, [128, TT], BF16, sC) for i in range(2)]
                              QR = [k.sbuf("QR%d" % i, [128, TT], BF16, sC) for i in range(2)]
                              T64 = [k.sbuf("T64%d" % i, [64, TT], F32, sC) for i in range(2)]
                              KN = [k.sbuf("KN%d" % i, [128, 1024], BF16, sC) for i in range(2)]
                              KRb = [k.sbuf("KRb%d" % i, [128, 1024], BF16, sC) for i in range(2)]
                              for b_ in QR + KRb:
                                  k.memset(b_[:, :], 0.0)
                              VV = [k.sbuf("VV%d" % i, [128, 1024], BF16, sC) for i in range(2)]
                              PT = [k.sbuf("PT%d" % i, [128, TT], BF16, sC) for i in range(3)]
                              SP32 = [k.sbuf("SP32%d" % i, [128, TT], F32, sC) for i in range(2)]
                              FTB = k.sbuf("FTB", [128, TT], F32, sC)
                              RD = k.sbuf("RD", [128, TT], F32, sC)
                              cnt = dict(kv=0, pt=0, sp=0)

                              def attention(kind, h, Qn, Qr, ot_chunk):
                                  rbase = R_KA if kind == "a" else R_KC
                                  vbase = R_VA if kind == "a" else R_VC
                                  kdim = 192 if kind == "a" else 128
                                  ncol = nJ * 128
                                  first = True
                                  for r in range(8):
                                      i3 = cnt["kv"] % 2
                                      cnt["kv"] += 1
                                      kn, krb, vv = KN[i3], KRb[i3], VV[i3]
                                      kbase = r * R16 + rbase + h * kdim
                                      k.dma("sp", kn[:, 0:ncol], V(g16.t[kbase:kbase + 128, 0:ncol], g16))
                                      if kind == "a":
                                          k.dma("sp", krb[0:64, 0:ncol], V(g16.t[kbase + 128:kbase + 192, 0:ncol], g16))
                                      vb0 = r * R16 + vbase + h * 128
                                      k.dma("sp", vv[:, 0:ncol], V(g16.t[vb0:vb0 + 128, 0:ncol], g16))
                                      for J in range(nJ):
                                          m0 = max(J, 4 * tt)
                                          c0 = (m0 - 4 * tt) * 128
                                          N = TT - c0
                                          ks = slice(J * 128, (J + 1) * 128)
                                          S = k.ps()
                                          k.mm(S[:, 0:N], kn[:, ks], Qn[:, c0:TT], start=True, stop=(kind != "a"))
                                          if kind == "a":
                                              k.mm(S[:, 0:N], krb[:, ks], Qr[:, c0:TT], start=False, stop=True)
                                          P = PT[cnt["pt"] % 3]
                                          cnt["pt"] += 1
                                          if kind == "c":
                                              sp = SP32[cnt["sp"] % 2]
                                              cnt["sp"] += 1
                                              k.tt(sp[:, 0:N], S[:, 0:N], FTB[:, c0:TT], ALU.add)
                                              bcol = (8 * J + r) * 8 + h
                                              if J >= 4 * tt:
                                                  k.ts(sp[:, 0:N], sp[:, 0:N], NEGF[:, bcol:bcol + 1], ALU.add, 60.0, ALU.min)
                                                  k.act(P[:, 0:N], sp[:, 0:N], AF.Exp)
                                              else:
                                                  k.act(P[:, 0:N], sp[:, 0:N], AF.Exp, bias=NEGF[:, bcol:bcol + 1])
                                          else:
                                              k.act(P[:, 0:N], S[:, 0:N], AF.Exp)
                                          if J >= 4 * tt:
                                              k.tt(P[:, 0:128], P[:, 0:128], MK[:, r, :], ALU.mult)
                                          last = (r == 7 and J == nJ - 1)
                                          k.mm(PSO[:, c0:TT], vv[:, ks], P[:, 0:N], start=first, stop=last)
                                          k.mm(PSD[:, c0:TT], ONESB[:, :], P[:, 0:N], start=first, stop=last)
                                          first = False
                                  k.recip(RD[:, :], PSD[:, :])
                                  k.tt(ot_chunk, PSO[:, :], RD[:, :], ALU.mult)

                              sc_a = 192.0 ** -0.5
                              for h in range(8):
                                  W = wload(WUQ[l][:, :, h * 192:(h + 1) * 192], 4, 192)
                                  qn_ps = k.ps()
                                  qr_ps = k.ps()
                                  qs_ps = k.ps()
                                  for kc in range(4):
                                      k.mm(qn_ps[:, :], W[:, kc, 0:128], CQN[:, kc, :], start=(kc == 0), stop=(kc == 3))
                                  for kc in range(4):
                                      k.mm(qr_ps[0:64, :], W[:, kc, 128:192], CQN[:, kc, :], start=(kc == 0), stop=(kc == 3))
                                  for kc in range(4):
                                      k.mm(qs_ps[0:32, :], W[:, kc, 160:192], CQN[:, kc, :], start=(kc == 0), stop=(kc == 3))
                                  for kc in range(4):
                                      k.mm(qs_ps[32:64, :], W[:, kc, 128:160], CQN[:, kc, :], start=(kc == 0), stop=(kc == 3))
                                  rh = rvb()
                                  rinv_of([(qn_ps[:, :], 128, False), (qr_ps[0:64, :], 64, False)], 192.0, rh)
                                  k.ts(rh[:, :], rh[:, :], sc_a, ALU.mult)
                                  qn, qr = QN[h % 2], QR[h % 2]
                                  t1, t2 = T64
                                  k.stt(qn[:, :], qn_ps[:, :], gcol(l, G_QN), rh[:, :], ALU.mult, ALU.mult)
                                  k.stt(t1[:, :], qr_ps[0:64, :], gcol(l, G_QR, 64), ROPC[:, tsl], ALU.mult, ALU.mult)
                                  k.stt(t2[:, :], qs_ps[0:64, :], gcol(l, G_QRS, 64), ROPS[:, tsl], ALU.mult, ALU.mult)
                                  k.tt(t1[:, :], t1[:, :], t2[:, :], ALU.add)
                                  k.tt(qr[0:64, :], t1[:, :], rh[0:64, :], ALU.mult)
                                  attention("a", h, qn, qr, OT[:, h, :])
                              sc_c = 128.0 ** -0.5
                              for h in range(8):
                                  W = wload(WIN[l][:, :, C_FQ + h * 128:C_FQ + (h + 1) * 128], KC, 128)
                                  q_ps = k.ps()
                                  for kc in range(KC):
                                      k.mm(q_ps[:, :], W[:, kc, :], hT(kc), start=(kc == 0), stop=(kc == KC - 1))
                                  rh = rvb()
                                  rinv_of([(q_ps[:, :], 128, False)], 128.0, rh)
                                  k.ts(rh[:, :], rh[:, :], sc_c, ALU.mult)
                                  qn = QN[h % 2]
                                  k.stt(qn[:, :], q_ps[:, :], gcol(l, G_FQ), rh[:, :], ALU.mult, ALU.mult)
                                  for mi in range(4):
                                      col = (4 * tt + mi) * 8 + h
                                      k.copy(FTB[:, mi * 128:(mi + 1) * 128], QCB.v(QCB.t[:, col:col + 1].to_broadcast([128, 128])))
                                  attention("c", h, qn, None, OT[:, 16 + h, :])
                              k.barrier()
                              ck("C%d" % tt)
                          with contextlib.ExitStack() as sG:
                              SPB = k.sbuf("SPB", [128, 4, 512], F32, sG)
                              GAA = k.sbuf("GAA", [17, TT], F32, sG)
                              VB = k.sbuf("VB", [128, 4, 1024], BF16, sG)
                              EPH = k.sbuf("EPH", [128, TT], F32, sG)
                              ENH = k.sbuf("ENH", [128, TT], F32, sG)
                              QT = k.sbuf("QT", [128, TT], BF16, sG)
                              KTt = k.sbuf("KTt", [128, TT], BF16, sG)
                              AT = k.sbuf("AT", [128, TT], BF16, sG)
                              OG = k.sbuf("OG", [128, 2, TT], F32, sG)
                              SG = k.sbuf("SG", [128, TT], F32, sG)
                              k.memset(GAA[:, :], 1.0)
                              gla_sp(l, HT, SPB, GAA)
                              gla_v(l, HT, VB)
                              sc_b = 128.0 ** -0.5
                              for h in range(4):
                                  bt = k.ps()
                                  for blk in range(4):
                                      k.mm(bt[:, blk * 128:(blk + 1) * 128], SPB[:, blk, h * 128:(h + 1) * 128], TRIS[:, :],
                                           start=True, stop=True)
                                  k.act(EPH[:, :], bt[:, :], AF.Exp)
                                  k.act(ENH[:, :], bt[:, :], AF.Exp, scale=-1.0)
                                  Wq = wload(WIN[l][:, :, C_GQ + h * 128:C_GQ + (h + 1) * 128], KC, 128)
                                  q_ps = k.ps()
                                  for kc in range(KC):
                                      k.mm(q_ps[:, :], Wq[:, kc, :], hT(kc), start=(kc == 0), stop=(kc == KC - 1))
                                  k.stt(QT[:, :], q_ps[:, :], sc_b, EPH[:, :], ALU.mult, ALU.mult)
                                  Wk2 = wload(WINK[l][:, :, cols["GK"] + h * 128:cols["GK"] + (h + 1) * 128], KC, 128)
                                  k_ps = k.ps()
                                  for kc in range(KC):
                                      k.mm(k_ps[:, :], Wk2[:, kc, :], hT(kc), start=(kc == 0), stop=(kc == KC - 1))
                                  k.tt(KTt[:, :], k_ps[:, :], ENH[:, :], ALU.mult)
                                  a_ps = k.ps()
                                  for blk in range(4):
                                      bs = slice(blk * 128, (blk + 1) * 128)
                                      k.mm(a_ps[:, bs], KTt[:, bs], QT[:, bs], start=True, stop=True)
                                  for blk in range(4):
                                      bs = slice(blk * 128, (blk + 1) * 128)
                                      k.tt(AT[:, bs], a_ps[:, bs], TRI[:, :], ALU.mult)
                                  for half in range(2):
                                      o_ps = k.ps()
                                      vs = slice(h * 256 + half * 128, h * 256 + (half + 1) * 128)
                                      for blk in range(4):
                                          bs = slice(blk * 128, (blk + 1) * 128)
                                          k.mm(o_ps[:, bs], VB[:, blk, vs], AT[:, bs], start=True, stop=False)
                                          k.mm(o_ps[:, bs], SOWN[:, 4 * tt + blk, vs], QT[:, bs], start=False, stop=True)
                                      k.copy(OG[:, half, :], o_ps[:, :], eng="act")
                                  rh = rvb()
                                  rinv_of([(OG[:, 0, :], 128, False), (OG[:, 1, :], 128, False)], 256.0, rh)
                                  Wr = wload(WIN[l][:, :, C_GR + h * 256:C_GR + (h + 1) * 256], KC, 256)
                                  for half in range(2):
                                      g_ps = k.ps()
                                      for kc in range(KC):
                                          k.mm(g_ps[:, :], Wr[:, kc, half * 128:(half + 1) * 128], hT(kc),
                                               start=(kc == 0), stop=(kc == KC - 1))
                                      k.act(SG[:, :], g_ps[:, :], AF.Silu)
                                      k.stt(OG[:, half, :], OG[:, half, :], gcol(l, G_GLA + half), rh[:, :], ALU.mult, ALU.mult)
                                      k.tt(OT[:, 8 + 2 * h + half, :], OG[:, half, :], SG[:, :], ALU.mult)
                              k.barrier()
                              ck("G%d" % tt)
                          with contextlib.ExitStack() as sD:
                              MT = k.sbuf("MT", [128, KC, TT], BF16, sD)
                              SGD = [k.sbuf("SGD%d" % i, [128, TT], F32, sD) for i in range(2)]
                              ACC = [k.sbuf("ACC%d" % i, [128, TT], F32, sD) for i in range(2)]
                              for dp in range(8):
                                  for n in range(3):
                                      Wbn = wload(WBR[l][n][:, :, dp * 256:(dp + 1) * 256], 8, 256)
                                      Wg = wload(WIN[l][:, :, C_GATES + n * D + dp * 256:C_GATES + n * D + (dp + 1) * 256], KC, 256)
                                      for ci in range(2):
                                          dc = dp * 2 + ci
                                          cs = slice(ci * 128, (ci + 1) * 128)
                                          y_ps = k.ps()
                                          for kc in range(8):
                                              k.mm(y_ps[:, :], Wbn[:, kc, cs], OT[:, n * 8 + kc, :], start=(kc == 0), stop=(kc == 7))
                                          g_ps = k.ps()
                                          for kc in range(KC):
                                              k.mm(g_ps[:, :], Wg[:, kc, cs], hT(kc), start=(kc == 0), stop=(kc == KC - 1))
                                          sg = SGD[(n * 2 + ci) % 2]
                                          k.act(sg[:, :], g_ps[:, :], AF.Sigmoid)
                                          acc = ACC[ci]
                                          if n == 0:
                                              k.tt(acc[:, :], y_ps[:, :], sg[:, :], ALU.mult)
                                          else:
                                              k.tt(sg[:, :], y_ps[:, :], sg[:, :], ALU.mult)
                                              if n == 1:
                                                  k.tt(acc[:, :], acc[:, :], sg[:, :], ALU.add)
                                              else:
                                                  k.tt(MT[:, dc, :], acc[:, :], sg[:, :], ALU.add)
                              def cons_out(ci, ps, m):
                                  k.tt(XT[:, ci, tsl], XT[:, ci, tsl], ps[:, :], ALU.add)
                              proj_fm(WOUT[l], 0, D, KC, lambda kc: MT[:, kc, :], cons_out)
                              k.barrier()
                              ck("D%d" % tt)
                  k.barrier()
                  sM.close()
                  with contextlib.ExitStack() as sF:
                      HT2 = k.sbuf("HT2", [128, KC, TOK], BF16, sF)
                      AH = k.sbuf("AH", [128, 22, TOK], BF16, sF)
                      SGF = [k.sbuf("SGF%d" % i, [128, TT], F32, sF) for i in range(2)]
                      for tt in range(2):
                          rmsnorm_x(l, tt, G_FFN, HT2, off=tt * TT)
                      nsg = 0
                      for half in range(2):
                          for hp in range(11):
                              col = (half * 22 + 2 * hp) * 128
                              Wg = wload(WGU[l][:, :, col:col + 256], KC, 256)
                              Wu = wload(WGU[l][:, :, HID + col:HID + col + 256], KC, 256)
                              for ci in range(2):
                                  cs = slice(ci * 128, (ci + 1) * 128)
                                  for tt in range(2):
                                      tsl = slice(tt * TT, (tt + 1) * TT)
                                      g_ps = k.ps()
                                      for kc in range(KC):
                                          k.mm(g_ps[:, :], Wg[:, kc, cs], HT2[:, kc, tsl], start=(kc == 0), stop=(kc == KC - 1))
                                      u_ps = k.ps()
                                      for kc in range(KC):
                                          k.mm(u_ps[:, :], Wu[:, kc, cs], HT2[:, kc, tsl], start=(kc == 0), stop=(kc == KC - 1))
                                      sg = SGF[nsg % 2]
                                      nsg += 1
                                      k.act(sg[:, :], g_ps[:, :], AF.Silu)
                                      k.tt(AH[:, 2 * hp + ci, tsl], sg[:, :], u_ps[:, :], ALU.mult)
                          for dc in range(KC):
                              W = wload(WDN[l][:, half * 22:(half + 1) * 22, dc * 128:(dc + 1) * 128], 22, 128)
                              for tt in range(2):
                                  tsl = slice(tt * TT, (tt + 1) * TT)
                                  ps = k.ps()
                                  for kc in range(22):
                                      k.mm(ps[:, :], W[:, kc, :], AH[:, kc, tsl], start=(kc == 0), stop=(kc == 21))
                                  k.tt(XT[:, dc, tsl], XT[:, dc, tsl], ps[:, :], ALU.add)
                      k.barrier()
                      ck("F")
                  k.barrier()

        except _Stop:
            pass
        k.stopped = False

        if isA:
            o16 = Buf(nc.dram_tensor("s16_out", [R16, 1024], BF16, kind="ExternalOutput"), "s16_out")
            o32 = Buf(nc.dram_tensor("s32_out", [R32, 1024], F32, kind="ExternalOutput"), "s32_out")
            o16.multi = True
            o32.multi = True
            k.barrier()
            for src, dst, nr in ((S16[0], o16, R16), (S32[0], o32, R32)):
                for r0 in range(0, nr, 512):
                    r1 = min(nr, r0 + 512)
                    k.dma("sp", dst[r0:r1, :], src[r0:r1, :], sembuf=dst)
        else:
            k.dma("sp", out_d[:, :, :], XT[:, :, :])
        k.barrier()
    return nc


def _host_consts(c):
    s = np.arange(128)[:, None]
    t = np.arange(128)[None, :]
    tri = (s <= t).astype(np.float32)
    mk = np.zeros((128, 8, 128), np.float32)
    for r in range(8):
        if r < c:
            mk[:, r, :] = 1.0
        elif r == c:
            mk[:, r, :] = tri
    cvec = np.zeros((128, 20), np.float32)
    for r in range(8):
        cvec[:, r] = 1.0 if r < c else 0.0
        cvec[:, 8 + r] = 1.0 if r == c else 0.0
    half = 32
    inv = (np.float32(10000.0) ** (-(np.arange(half, dtype=np.float32) / np.float32(half)))).astype(np.float32)
    cvec[0:64, 16] = np.concatenate([inv, inv])
    cvec[0:64, 17] = np.concatenate([-np.ones(32, np.float32), np.ones(32, np.float32)])
    return dict(
        mk=mk.astype(ml_dtypes.bfloat16), tri=tri.astype(ml_dtypes.bfloat16),
        tris=(tri * np.float32(-1.0 / 16.0)).astype(np.float32), tril1=tri.astype(np.float32), cvec=cvec)


def _pack_gains(inp):
    g = np.zeros((128, DEPTH * GL), np.float32)
    for l in range(DEPTH):
        o = l * GL
        g[:, o + G_MIX:o + G_MIX + 16] = inp["g_mix"][l].reshape(16, 128).T
        g[:, o + G_FFN:o + G_FFN + 16] = inp["g_ffn"][l].reshape(16, 128).T
        g[:, o + G_CQ:o + G_CQ + 4] = inp["g_cq"][l].reshape(4, 128).T
        g[:, o + G_CKV:o + G_CKV + 4] = inp["g_ckv"][l].reshape(4, 128).T
        for nm, cn, cr, cs in (("g_mla_q", G_QN, G_QR, G_QRS), ("g_mla_k", G_KN, G_KR, G_KRS)):
            v = inp[nm][l]
            g[:, o + cn] = v[0:128]
            g[0:64, o + cr] = v[128:192]
            g[0:64, o + cs] = np.concatenate([v[160:192], v[128:160]])
        g[:, o + G_GLA:o + G_GLA + 2] = inp["g_gla_o"][l].reshape(2, 128).T
        g[:, o + G_FQ] = inp["g_fox_q"][l]
        g[:, o + G_FK] = inp["g_fox_k"][l]
        g[:, o + G_BF:o + G_BF + 8] = np.broadcast_to(inp["b_f"][l][None, :], (128, 8))
    return g


_KCOLS = np.concatenate([np.arange(512, 1024), np.arange(1024, 1088), np.arange(1600, 2112), np.arange(2112, 3136),
                         np.arange(3136, 3152), np.arange(5200, 6224), np.arange(6224, 7248), np.arange(7248, 7256)])


def _f32(a):
    return np.ascontiguousarray(a, dtype=np.float32)


def _core_common(inp, c, x_c, lsl):
    gains = _pack_gains(inp)
    if lsl.start:
        gains = np.roll(gains, -lsl.start * GL, axis=1)
    pos = np.asarray(inp["positions"])[0].reshape(8, 8, 128)
    w_a2aug = np.concatenate([inp["w_a2"], inp["b_a"][:, None, :]], axis=1)
    m = dict(gains=_f32(gains), w_a2aug=_f32(w_a2aug[lsl]))
    m.update(_host_consts(c))
    m["xT"] = x_c
    m["pos"] = np.ascontiguousarray(np.broadcast_to(pos[:, c, :].reshape(1, TOK), (64, TOK)).astype(np.int32))
    return m


def _weights_A(inp, lsl):
    return dict(w_ink=_f32(inp["w_in"][lsl][:, :, _KCOLS]), w_ukv=_f32(inp["w_ukv"][lsl]))


def _weights_B(inp, lsl):
    return dict(w_in=_f32(inp["w_in"][lsl]), w_uq=_f32(inp["w_uq"][lsl]), w_branch=_f32(inp["w_branch"][lsl]),
                w_out=_f32(inp["w_out"][lsl]), w_gu=_f32(inp["w_gu"][lsl]), w_down=_f32(inp["w_down"][lsl]))


def _run_fused(nc, x_cores, inp):
    lsl = slice(0, DEPTH)
    wa = _weights_A(inp, lsl)
    wb = _weights_B(inp, lsl)
    shared = dict(wb)
    shared["w_ukv"] = wa["w_ukv"]
    in_maps = []
    for c in range(NCORES):
        m = _core_common(inp, c, x_cores[c], lsl)
        m.update(shared)
        in_maps.append(m)
    res = run_bass_kernel_spmd(nc, in_maps, core_ids=list(range(NCORES)))
    return [np.asarray(res.results[c]["outT"]) for c in range(NCORES)]


def _run_layer(ncA, ncB, x_cores, inp, l):
    lsl = slice(l, l + 1)
    wa = _weights_A(inp, lsl)
    maps = []
    for c in range(NCORES):
        m = _core_common(inp, c, x_cores[c], lsl)
        m.update(wa)
        maps.append(m)
    resA = run_bass_kernel_spmd(ncA, maps, core_ids=list(range(NCORES)))
    g16 = np.ascontiguousarray(np.concatenate([np.asarray(resA.results[c]["s16_out"]) for c in range(NCORES)], axis=0))
    g32 = np.ascontiguousarray(np.concatenate([np.asarray(resA.results[c]["s32_out"]) for c in range(NCORES)], axis=0))
    wb = _weights_B(inp, lsl)
    maps = []
    for c in range(NCORES):
        m = _core_common(inp, c, x_cores[c], lsl)
        m.update(wb)
        m["g16_0"] = g16
        m["g32_0"] = g32
        maps.append(m)
    resB = run_bass_kernel_spmd(ncB, maps, core_ids=list(range(NCORES)))
    return [np.asarray(resB.results[c]["outT"]) for c in range(NCORES)]


def _to_cores(x):
    x = x.reshape(8, 8, 128, D)
    out = []
    for c in range(NCORES):
        xs = x[:, c].reshape(TOK, D)
        out.append(np.ascontiguousarray(xs.T.reshape(KC, 128, TOK).transpose(1, 0, 2)))
    return out


def _from_cores(x_cores):
    out = np.empty((8, 8, 128, D), np.float32)
    for c in range(NCORES):
        xs = x_cores[c].transpose(1, 0, 2).reshape(D, TOK).T
        out[:, c] = xs.reshape(8, 128, D)
    return out.reshape(1, 8192, D)


def kernel(**inputs):
    inp = {k_: np.asarray(v) for k_, v in inputs.items()}
    x_cores = _to_cores(inp["x"][0])
    if FUSED:
        nc = build(DEPTH, "fused")
        x_cores = _run_fused(nc, x_cores, inp)
    else:
        ncA = build(1, "A")
        ncB = build(1, "B")
        for l in range(DEPTH):
            x_cores = _run_layer(ncA, ncB, x_cores, inp, l)
    return _from_cores(x_cores)
```

```python
import contextlib
import math
import numpy as np
import ml_dtypes
import concourse.bass as bass
import concourse.mybir as mybir
from concourse.bass_utils import run_bass_kernel_spmd

F32 = mybir.dt.float32
BF16 = mybir.dt.bfloat16
I32 = mybir.dt.int32
AF = mybir.ActivationFunctionType
ALU = mybir.AluOpType
AX = mybir.AxisListType

NCORES = 8
DEPTH = 4
D = 2048
KC = 16
TOK = 1024
TT = 512
DIN = 13400
HID = 5632
EPS = 1e-6
C_CQ, C_CKV, C_KR, C_GQ, C_GK, C_GV, C_GA, C_GR, C_FQ, C_FK, C_FV, C_FL, C_GATES = (
    0, 512, 1024, 1088, 1600, 2112, 3136, 3152, 4176, 5200, 6224, 7248, 7256)
R_KA, R_VA, R_KC, R_VC, R16 = 0, 1536, 2560, 3584, 4608
R_LF, R_UX, R_DX, R32 = 0, 8, 1032, 1152
G_MIX, G_FFN, G_CQ, G_CKV, G_QN, G_QR, G_QRS, G_KN, G_KR, G_KRS, G_GLA, G_FQ, G_FK, G_BF, GL = (
    0, 16, 32, 36, 40, 41, 42, 43, 44, 45, 46, 48, 49, 50, 58)
FUSED = False


class V:
    __slots__ = ("ap", "buf")

    def __init__(self, ap, buf):
        self.ap = ap
        self.buf = buf


class Buf:
    def __init__(self, t, name):
        self.t = t
        self.name = name
        self.w = {}
        self.r = {}
        self.dkey = None
        self.multi = False
        self.psum = False

    def __getitem__(self, idx):
        return V(self.t[idx], self)

    def v(self, ap):
        return V(ap, self)


class WView:
    def __init__(self, buf, ap):
        self.buf = buf
        self.ap3 = ap

    def __getitem__(self, idx):
        return V(self.ap3[idx], self.buf)


class KB:
    def __init__(self, nc, es):
        self.nc = nc
        self.es = es
        self.E = dict(pe=nc.tensor, act=nc.scalar, dve=nc.vector, pool=nc.gpsimd, sp=nc.sync)
        self.sem = {}
        self.cnt = {}
        self.ekey = {}
        self.epoch = -1
        self.known = {e: {} for e in self.E}
        self.nd = 0
        self.nbuf = 0
        self.psums = []
        self.psi = 0
        self.stopped = False
        self.free_dsems = []
        self.new_epoch()

    def new_epoch(self):
        self.epoch += 1
        for e in list(self.E) + ["cc"]:
            key = "%s@%d" % (e, self.epoch)
            self.sem[key] = self.es.enter_context(self.nc.semaphore("s_%s_%d" % (e, self.epoch)))
            self.cnt[key] = 0
            self.ekey[e] = key

    def release(self, buf):
        if buf.dkey is not None:
            self.free_dsems.append(buf.dkey)
            buf.dkey = None

    def sbuf(self, name, shape, dtype, es=None):
        self.nbuf += 1
        t = (es or self.es).enter_context(self.nc.sbuf_tensor("%s_%d" % (name, self.nbuf), shape, dtype))
        b = Buf(t, name)
        if es is not None:
            es.callback(self.release, b)
        return b

    def dram(self, name, shape, dtype):
        t = self.nc.dram_tensor(name, shape, dtype)
        return Buf(t, name)

    def init_psum(self, nrot):
        for i in range(8):
            t = self.es.enter_context(self.nc.psum_tensor("ps%d" % i, [128, 512], F32))
            self.psums.append(Buf(t, "ps%d" % i))
            self.psums[-1].psum = True
        self.nrot = nrot

    def ps(self):
        b = self.psums[self.psi % self.nrot]
        self.psi += 1
        return b

    def _dkey(self, buf):
        if buf.dkey is None:
            if self.free_dsems:
                key = self.free_dsems.pop()
            else:
                self.nd += 1
                key = "d%d" % self.nd
                self.sem[key] = self.es.enter_context(self.nc.semaphore(key))
                self.cnt[key] = 0
            buf.dkey = key
        return buf.dkey

    def _wait(self, eng, key, val):
        if val <= self.known[eng].get(key, 0):
            return
        self.known[eng][key] = val
        self.E[eng].wait_ge(self.sem[key], val)

    def _deps(self, eng, rb, wb, extra=()):
        deps = {}

        def add(tok):
            if tok is None:
                return
            k_, v_ = tok
            if eng == "pe" and k_.startswith("pe@"):
                return
            if deps.get(k_, 0) < v_:
                deps[k_] = v_

        for b in rb:
            for k_, v_ in b.w.items():
                add((k_, v_))
            if b.psum:
                for k_, v_ in b.r.items():
                    if k_ != self.ekey[eng]:
                        add((k_, v_))
        for b in wb:
            if b.multi:
                continue
            for k_, v_ in b.w.items():
                add((k_, v_))
            for k_, v_ in b.r.items():
                add((k_, v_))
        for t in extra:
            add(t)
        for k_, v_ in deps.items():
            self._wait(eng, k_, v_)

    def _commit(self, tok, rb, wb):
        for b in wb:
            if b.multi:
                if b.w.get(tok[0], 0) < tok[1]:
                    b.w[tok[0]] = tok[1]
            else:
                b.w = {tok[0]: tok[1]}
                b.r = {}
        for b in rb:
            if b in wb:
                continue
            if b.r.get(tok[0], 0) < tok[1]:
                b.r[tok[0]] = tok[1]

    @staticmethod
    def _bufs(vs):
        out = []
        for v in vs:
            if v is None or isinstance(v, (int, float)):
                continue
            if v.buf is not None and v.buf not in out:
                out.append(v.buf)
        return out

    def op(self, eng, fn, reads, writes):
        if self.stopped:
            return
        rb = self._bufs(reads)
        wb = self._bufs(writes)
        self._deps(eng, rb, wb)
        inst = fn(self.E[eng])
        ek = self.ekey[eng]
        self.cnt[ek] += 1
        inst.then_inc(self.sem[ek], 1)
        self._commit((ek, self.cnt[ek]), rb, wb)

    def dma(self, q, out, in_, sembuf=None):
        if self.stopped:
            return
        if sembuf is not None:
            sb = sembuf
        elif out.buf is not None and not (out.buf.multi and in_.buf is not None):
            sb = out.buf
        else:
            sb = in_.buf
        key = self._dkey(sb)
        rb = self._bufs([in_])
        wb = self._bufs([out])
        prev = (key, self.cnt[key]) if self.cnt[key] else None
        self._deps(q, rb, wb, extra=(prev,))
        self.E[q].dma_start(out=out.ap, in_=in_.ap).then_inc(self.sem[key], 16)
        self.cnt[key] += 16
        self._commit((key, self.cnt[key]), rb, wb)

    def allgather(self, send, recv):
        if self.stopped:
            return
        rb = [send]
        wb = [recv]
        self._deps("pool", rb, wb)
        self.nc.gpsimd.collective_compute(
            "AllGather", ALU.bypass, replica_groups=[list(range(NCORES))],
            ins=[send.t.ap().opt()], outs=[recv.t.ap().opt()]).then_inc(self.sem[self.ekey["cc"]])
        ck_ = self.ekey["cc"]
        self.cnt[ck_] += 1
        self._commit((ck_, self.cnt[ck_]), rb, wb)

    def barrier(self, engines=None):
        if self.stopped:
            return
        for e in (engines or self.E):
            for key, c in self.cnt.items():
                if c:
                    self._wait(e, key, c)

    def mm(self, out, lhsT, rhs, start, stop):
        self.op("pe", lambda E: E.matmul(out.ap, lhsT=lhsT.ap, rhs=rhs.ap, start=start, stop=stop),
                [lhsT, rhs] + ([] if start else [out]), [out])

    def act(self, out, in_, func, bias=None, scale=1.0):
        kw = {}
        if bias is not None:
            kw["bias"] = bias.ap if isinstance(bias, V) else bias
        self.op("act", lambda E: E.activation(out=out.ap, in_=in_.ap, func=func, scale=scale, **kw),
                [in_, bias], [out])

    def tt(self, out, in0, in1, op, eng="dve"):
        self.op(eng, lambda E: E.tensor_tensor(out=out.ap, in0=in0.ap, in1=in1.ap, op=op), [in0, in1], [out])

    def ts(self, out, in0, s1, op0, s2=None, op1=None, eng="dve"):
        a1 = s1.ap if isinstance(s1, V) else s1
        a2 = s2.ap if isinstance(s2, V) else s2
        if op1 is None:
            fn = lambda E: E.tensor_scalar(out=out.ap, in0=in0.ap, scalar1=a1, scalar2=None, op0=op0)
        else:
            fn = lambda E: E.tensor_scalar(out=out.ap, in0=in0.ap, scalar1=a1, scalar2=a2, op0=op0, op1=op1)
        self.op(eng, fn, [in0, s1, s2], [out])

    def stt(self, out, in0, scalar, in1, op0, op1):
        a = scalar.ap if isinstance(scalar, V) else scalar
        self.op("dve", lambda E: E.scalar_tensor_tensor(out=out.ap, in0=in0.ap, scalar=a, in1=in1.ap,
                                                        op0=op0, op1=op1), [in0, scalar, in1], [out])

    def copy(self, out, in_, eng="dve"):
        if eng == "act":
            self.op(eng, lambda E: E.activation(out=out.ap, in_=in_.ap, func=AF.Copy), [in_], [out])
        else:
            self.op(eng, lambda E: E.tensor_copy(out=out.ap, in_=in_.ap), [in_], [out])

    def memset(self, out, val, eng="dve"):
        self.op(eng, lambda E: E.memset(out.ap, val), [], [out])

    def recip(self, out, in_):
        self.op("dve", lambda E: E.reciprocal(out=out.ap, in_=in_.ap), [in_], [out])


class _Stop(Exception):
    pass


class _Dummy:
    def __getitem__(self, idx):
        return self

    def rearrange(self, *a, **kw):
        return self

    def ap(self):
        return self

    def opt(self):
        return self


def build(nlayers, mode="fused"):
    import os
    stop_at = os.environ.get("K_STOP", "")

    kref = []

    def ck(name):
        if stop_at == name and not kref[0].stopped:
            kref[0].barrier()
            kref[0].stopped = True
    nc = bass.Bass("TRN2", target_bir_lowering=False)
    es = contextlib.ExitStack()
    with es:
        k = KB(nc, es)
        kref.append(k)

        def ext_in(name, shape, dt):
            return Buf(nc.dram_tensor(name, shape, dt, kind="ExternalInput"), name)

        xT_d = ext_in("xT", [128, KC, TOK], F32)
        pos_d = ext_in("pos", [64, TOK], I32)
        gains_d = ext_in("gains", [128, DEPTH * GL], F32)
        mk_d = ext_in("mk", [128, 8, 128], BF16)
        tri_d = ext_in("tri", [128, 128], BF16)
        tris_d = ext_in("tris", [128, 128], F32)
        tril1_d = ext_in("tril1", [128, 128], F32)
        cvec_d = ext_in("cvec", [128, 20], F32)
        def wdecl(name, shape, used):
            if used:
                return ext_in(name, shape, F32)
            return Buf(_Dummy(), name)
        isA, isB, isF = mode == "A", mode == "B", mode == "fused"
        if isA:
            cols = dict(CKV=0, KR=512, GK=576, GV=1088, GA=2112, FK=2128, FV=3152, FL=4176)
            w_ink_d = ext_in("w_ink", [nlayers, D, 4184], F32)
        else:
            cols = dict(CKV=C_CKV, KR=C_KR, GK=C_GK, GV=C_GV, GA=C_GA, FK=C_FK, FV=C_FV, FL=C_FL)
        w_in_d = wdecl("w_in", [nlayers, D, DIN], not isA)
        if not isA:
            w_ink_d = w_in_d
        w_uq_d = wdecl("w_uq", [nlayers, 512, 1536], not isA)
        w_ukv_d = wdecl("w_ukv", [nlayers, 512, 2048], not isB)
        w_a2_d = ext_in("w_a2aug", [nlayers, 17, 512], F32)
        w_br_d = wdecl("w_branch", [nlayers, 3, 1024, D], not isA)
        w_out_d = wdecl("w_out", [nlayers, D, D], not isA)
        w_gu_d = wdecl("w_gu", [nlayers, D, 2 * HID], not isA)
        w_dn_d = wdecl("w_down", [nlayers, HID, D], not isA)
        if not isA:
            out_d = Buf(nc.dram_tensor("outT", [128, KC, TOK], F32, kind="ExternalOutput"), "outT")
        for b in (xT_d, pos_d, gains_d, mk_d, tri_d, tris_d, tril1_d, cvec_d):
            pass
        WIN = [w_in_d.t[l].rearrange("(kc p) n -> p kc n", p=128) for l in range(nlayers)]
        WINK = [w_ink_d.t[l].rearrange("(kc p) n -> p kc n", p=128) for l in range(nlayers)]
        WUQ = [w_uq_d.t[l].rearrange("(kc p) n -> p kc n", p=128) for l in range(nlayers)]
        WUKV = [w_ukv_d.t[l].rearrange("(kc p) n -> p kc n", p=128) for l in range(nlayers)]
        WBR = [[w_br_d.t[l, n].rearrange("(kc p) n -> p kc n", p=128) for n in range(3)] for l in range(nlayers)]
        WOUT = [w_out_d.t[l].rearrange("(kc p) n -> p kc n", p=128) for l in range(nlayers)]
        WGU = [w_gu_d.t[l].rearrange("(kc p) n -> p kc n", p=128) for l in range(nlayers)]
        WDN = [w_dn_d.t[l].rearrange("(kc p) n -> p kc n", p=128) for l in range(nlayers)]

        def xbuf(name, shape, dt, kind):
            if kind == "none":
                return Buf(_Dummy(), name)
            if kind is None:
                return k.dram(name, shape, dt)
            return Buf(nc.dram_tensor(name, shape, dt, kind=kind), name)
        sk = "none" if isB else None
        gk_ = "ExternalInput" if isB else ("none" if isA else None)
        nset = min(nlayers, 2)
        S16 = [xbuf("s16_%d" % l, [R16, 1024], BF16, sk) for l in range(nset)]
        G16 = [xbuf("g16_%d" % l, [8 * R16, 1024], BF16, gk_) for l in range(nset)]
        S32 = [xbuf("s32_%d" % l, [R32, 1024], F32, sk) for l in range(nset)]
        G32 = [xbuf("g32_%d" % l, [8 * R32, 1024], F32, gk_) for l in range(nset)]
        for b in S16 + S32:
            b.multi = True

        k.init_psum(6)
        PSO = k.psums[6]
        PSD = k.psums[7]

        XT = k.sbuf("XT", [128, KC, TOK], F32)
        GN = k.sbuf("GN", [128, DEPTH * GL], F32)
        MK = k.sbuf("MK", [128, 8, 128], BF16)
        TRI = k.sbuf("TRI", [128, 128], BF16)
        TRIS = k.sbuf("TRIS", [128, 128], F32)
        TRIL1 = k.sbuf("TRIL1", [128, 128], F32)
        CVEC = k.sbuf("CVEC", [128, 20], F32)
        ONESB = k.sbuf("ONESB", [128, 128], BF16)
        ONESF = k.sbuf("ONESF", [128, 128], F32)
        ROPC = k.sbuf("ROPC", [64, TOK], F32)
        ROPS = k.sbuf("ROPS", [64, TOK], F32)
        WR = [k.sbuf("WR%d" % i, [128, 4096], BF16) for i in range(3)]
        SQ = [k.sbuf("SQ%d" % i, [128, TT], BF16) for i in range(2)]
        RV = [k.sbuf("RV%d" % i, [128, TT], F32) for i in range(3)]
        state = dict(wi=0, sq=0, rv=0)

        def sqb():
            state["sq"] += 1
            return SQ[state["sq"] % 2]

        def rvb():
            state["rv"] += 1
            return RV[state["rv"] % 3]

        def wload(src_ap, kcn, n, split=None):
            slot = WR[state["wi"] % 3]
            state["wi"] += 1
            v3 = slot.t[:, 0:kcn * n].rearrange("p (k n) -> p k n", k=kcn)
            if split is None:
                k.dma("pool", V(v3, slot), V(src_ap, None))
            else:
                dst = slot.t[:, 0:kcn * n].rearrange("p (k h c) -> p k h c", k=kcn, h=split)
                for kc in range(kcn):
                    k.dma("pool", V(dst[:, kc], slot), V(src_ap[:, kc], None))
            return WView(slot, v3)

        k.dma("sp", XT[:, :, :], xT_d[:, :, :])
        k.dma("sp", GN[:, :], gains_d[:, :])
        k.dma("sp", MK[:, :, :], mk_d[:, :, :])
        k.dma("sp", TRI[:, :], tri_d[:, :])
        k.dma("sp", TRIS[:, :], tris_d[:, :])
        k.dma("sp", TRIL1[:, :], tril1_d[:, :])
        k.dma("sp", CVEC[:, :], cvec_d[:, :])
        k.memset(ONESB[:, :], 1.0)
        k.memset(ONESF[:, :], 1.0)
        MASKR = lambda r: CVEC[:, r:r + 1]
        OH = lambda r: CVEC[:, 8 + r:9 + r]

        with contextlib.ExitStack() as s0:
            POSI = k.sbuf("POSI", [64, TOK], I32, s0)
            ANG = k.sbuf("ANG", [64, TOK], F32, s0)
            T1 = k.sbuf("T1", [64, TOK], F32, s0)
            T2 = k.sbuf("T2", [64, TOK], F32, s0)
            k.dma("sp", POSI[:, :], pos_d[:, :])
            k.copy(ANG[:, :], POSI[:, :])
            k.ts(ANG[:, :], ANG[:, :], CVEC[0:64, 16:17], ALU.mult)
            MAGIC = 12582912.0
            C1 = 6.28125
            C2 = 2.0 * math.pi - 6.28125
            for which, dst in ((0, ROPS), (1, ROPC)):
                k.ts(T1[:, :], ANG[:, :], 1.0 / (2.0 * math.pi), ALU.mult, (0.25 if which else 0.0), ALU.add)
                k.ts(T1[:, :], T1[:, :], MAGIC, ALU.add)
                k.ts(T1[:, :], T1[:, :], -MAGIC, ALU.add)
                k.stt(T2[:, :], T1[:, :], -C1, ANG[:, :], ALU.mult, ALU.add)
                k.stt(T2[:, :], T1[:, :], -C2, T2[:, :], ALU.mult, ALU.add)
                if which:
                    k.ts(T2[:, :], T2[:, :], math.pi / 2.0, ALU.add)
                k.ts(T2[:, :], T2[:, :], 3.1415925, ALU.min, -3.1415925, ALU.max)
                k.act(dst[:, :], T2[:, :], AF.Sin)
            k.ts(ROPS[:, :], ROPS[:, :], CVEC[0:64, 17:18], ALU.mult)
            k.barrier()

        def rinv_of(parts, nfeat, out):
            ss = k.ps()
            n = len(parts)
            for i, (p, rows, pre) in enumerate(parts):
                if pre:
                    s = p
                else:
                    sq = sqb()
                    k.act(sq[0:rows, :], p, AF.Square)
                    s = sq[0:rows, :]
                k.mm(ss[:, :], ONESB[0:rows, :], s, start=(i == 0), stop=(i == n - 1))
            k.act(out[:, :], ss[:, :], AF.Ln, bias=EPS, scale=1.0 / nfeat)
            k.act(out[:, :], out[:, :], AF.Exp, scale=-0.5)

        def rmsnorm_x(l, tt, gcol, HT, off=0):
            tsl = slice(tt * TT, (tt + 1) * TT)
            ss = k.ps()
            for kc in range(KC):
                sq = sqb()
                k.act(sq[:, :], XT[:, kc, tsl], AF.Square)
                k.mm(ss[:, :], ONESB[:, :], sq[:, :], start=(kc == 0), stop=(kc == KC - 1))
            rs = rvb()
            k.act(rs[:, :], ss[:, :], AF.Ln, bias=EPS, scale=1.0 / D)
            k.act(rs[:, :], rs[:, :], AF.Exp, scale=-0.5)
            for kc in range(KC):
                k.stt(HT[:, kc, off:off + TT], XT[:, kc, tsl], GN[:, l * GL + gcol + kc:l * GL + gcol + kc + 1], rs[:, :],
                      ALU.mult, ALU.mult)

        def proj_fm(wsrc, col0, ncols, kcn, rhs_fn, consume):
            for p0 in range(0, ncols, 256):
                pn = min(256, ncols - p0)
                W = wload(wsrc[:, :, col0 + p0:col0 + p0 + pn], kcn, pn)
                for c in range(0, pn, 128):
                    m = min(128, pn - c)
                    ps = k.ps()
                    for kc in range(kcn):
                        k.mm(ps[0:m, :], W[:, kc, c:c + m], rhs_fn(kc), start=(kc == 0), stop=(kc == kcn - 1))
                    consume((p0 + c) // 128, ps, m)

        def proj_tm(wsrc_fn, ncols, kcn, lhs_fn, consume):
            for p0 in range(0, ncols, 256):
                pn = min(256, ncols - p0)
                W = wload(wsrc_fn(p0, pn), kcn, pn)
                for blk in range(4):
                    ps = k.ps()
                    for kc in range(kcn):
                        k.mm(ps[:, 0:pn], lhs_fn(kc, blk), W[:, kc, 0:pn], start=(kc == 0), stop=(kc == kcn - 1))
                    consume(blk, p0, pn, ps)

        def gcol(l, c, rows=128):
            return GN[0:rows, l * GL + c:l * GL + c + 1]

        def gla_sp(l, HT, SPB, GAA):
            def cons(ci, ps, m):
                k.copy(GAA[0:16, :], ps[0:16, :])
            proj_fm(WINK[l], cols["GA"], 16, KC, lambda kc: HT[:, kc, :], cons)
            for blk in range(4):
                z = k.ps()
                k.mm(z[:, :], GAA[0:17, blk * 128:(blk + 1) * 128], WA2[0:17, :], start=True, stop=True)
                k.act(SPB[:, blk, :], z[:, :], AF.Exp, scale=-1.0)
                k.act(SPB[:, blk, :], SPB[:, blk, :], AF.Ln, bias=1.0)

        def gla_v(l, HT, VB):
            def cons(blk, p0, pn, ps):
                k.copy(VB[:, blk, p0:p0 + pn], ps[:, 0:pn], eng="act" if blk % 2 else "dve")
            proj_tm(lambda p0, pn: WINK[l][:, :, cols["GV"] + p0:cols["GV"] + p0 + pn], 1024, KC,
                    lambda kc, blk: HT[:, kc, blk * 128:(blk + 1) * 128], cons)

        try:
          ck("const")
          for l in range(nlayers):
              s16 = S16[l % 2]
              g16 = G16[l % 2]
              s32 = S32[l % 2]
              g32 = G32[l % 2]
              if l > 0:
                  for b_ in (s16, s32):
                      b_.w = {}
                      b_.r = {}
                  k.new_epoch()
              with contextlib.ExitStack() as sL:
                  WA2 = k.sbuf("WA2", [17, 512], F32, sL)
                  k.dma("sp", WA2[:, :], V(w_a2_d.t[l], None))

                  if isB:
                      k.stopped = True
                  for tt in range(2):
                      tsl = slice(tt * TT, (tt + 1) * TT)
                      with contextlib.ExitStack() as sA:
                          HT = k.sbuf("HT", [128, KC, TT], BF16, sA)
                          RAW = k.sbuf("RAW", [128, 4, TT], F32, sA)
                          CKN = k.sbuf("CKN", [128, 4, TT], BF16, sA)
                          KRR = k.sbuf("KRR", [64, TT], F32, sA)
                          KRQ = k.sbuf("KRQ", [64, TT], BF16, sA)
                          TMP = k.sbuf("TMP", [64, TT], F32, sA)
                          KO = [k.sbuf("KO%d" % i, [128, TT], BF16, sA) for i in range(2)]
                          KRO = [k.sbuf("KRO%d" % i, [64, TT], BF16, sA) for i in range(2)]
                          VO = [k.sbuf("VO%d" % i, [128, 1024], BF16, sA) for i in range(2)]
                          rmsnorm_x(l, tt, G_MIX, HT)
                          hT = lambda kc: HT[:, kc, :]

                          def cons_ckv(ci, ps, m):
                              k.copy(RAW[:, ci, :], ps[:, :], eng="act")
                          proj_fm(WINK[l], cols["CKV"], 512, KC, hT, cons_ckv)
                          rl = rvb()
                          rinv_of([(RAW[:, j, :], 128, False) for j in range(4)], 512.0, rl)
                          for j in range(4):
                              k.stt(CKN[:, j, :], RAW[:, j, :], gcol(l, G_CKV + j), rl[:, :], ALU.mult, ALU.mult)
                          ck("A%da" % tt)
                          kr_ps = k.ps()
                          krs_ps = k.ps()
                          Wk = wload(WINK[l][:, :, cols["KR"]:cols["KR"] + 64], KC, 64)
                          for kc in range(KC):
                              k.mm(kr_ps[0:64, :], Wk[:, kc, 0:64], hT(kc), start=(kc == 0), stop=(kc == KC - 1))
                          ck("A%da1" % tt)
                          for kc in range(KC):
                              k.mm(krs_ps[0:32, :], Wk[:, kc, 32:64], hT(kc), start=(kc == 0), stop=(kc == KC - 1))
                          for kc in range(KC):
                              k.mm(krs_ps[32:64, :], Wk[:, kc, 0:32], hT(kc), start=(kc == 0), stop=(kc == KC - 1))
                          ck("A%da2" % tt)
                          k.act(KRQ[:, :], kr_ps[0:64, :], AF.Square)
                          k.stt(KRR[:, :], kr_ps[0:64, :], gcol(l, G_KR, 64), ROPC[:, tsl], ALU.mult, ALU.mult)
                          k.stt(TMP[:, :], krs_ps[0:64, :], gcol(l, G_KRS, 64), ROPS[:, tsl], ALU.mult, ALU.mult)
                          k.tt(KRR[:, :], KRR[:, :], TMP[:, :], ALU.add)
                          ck("A%db" % tt)
                          KA_v = s16.t[R_KA:R_KA + 1536, :].rearrange("(h d) t -> h d t", h=8)
                          for h in range(8):
                              W = wload(WUKV[l][:, :, h * 256:h * 256 + 128], 4, 128)
                              ps = k.ps()
                              for kc in range(4):
                                  k.mm(ps[:, :], W[:, kc, :], CKN[:, kc, :], start=(kc == 0), stop=(kc == 3))
                              rh = rvb()
                              rinv_of([(ps[:, :], 128, False), (KRQ[:, :], 64, True)], 192.0, rh)
                              ko = KO[h % 2]
                              kro = KRO[h % 2]
                              k.stt(ko[:, :], ps[:, :], gcol(l, G_KN), rh[:, :], ALU.mult, ALU.mult)
                              k.tt(kro[:, :], KRR[:, :], rh[0:64, :], ALU.mult)
                              k.dma("sp", V(KA_v[h, 0:128, tsl], s16), ko[:, :])
                              k.dma("sp", V(KA_v[h, 128:192, tsl], s16), kro[:, :])
                          ck("A%dc" % tt)
                          Wv = wload(WUKV[l].rearrange("p k (h two c) -> p k h two c", h=8, two=2)[:, :, :, 1, :], 4, 1024, split=8)
                          VA_v = s16.t[R_VA:R_VA + 1024, :].rearrange("(h t) (j v) -> t j h v", h=8, j=8)
                          for blk in range(4):
                              vo = VO[blk % 2]
                              for half in range(2):
                                  ps = k.ps()
                                  for kc in range(4):
                                      k.mm(ps[:, :], CKN[:, kc, blk * 128:(blk + 1) * 128],
                                           Wv[:, kc, half * 512:(half + 1) * 512], start=(kc == 0), stop=(kc == 3))
                                  k.copy(vo[:, half * 512:(half + 1) * 512], ps[:, :], eng="act" if half else "dve")
                              k.dma("sp", V(VA_v[:, 4 * tt + blk, :, :], s16),
                                    vo.v(vo.t[:, :].rearrange("t (h v) -> t h v", h=8)))
                          ck("A%dd" % tt)
                          KC_v = s16.t[R_KC:R_KC + 1024, :].rearrange("(h d) t -> h d t", h=8)

                          def cons_fk(ci, ps, m):
                              rh = rvb()
                              rinv_of([(ps[:, :], 128, False)], 128.0, rh)
                              ko = KO[ci % 2]
                              k.stt(ko[:, :], ps[:, :], gcol(l, G_FK), rh[:, :], ALU.mult, ALU.mult)
                              k.dma("sp", V(KC_v[ci, :, tsl], s16), ko[:, :])
                          proj_fm(WINK[l], cols["FK"], 1024, KC, hT, cons_fk)
                          ck("A%de" % tt)
                          VC_v = s16.t[R_VC:R_VC + 1024, :].rearrange("(h t) (j v) -> t j h v", h=8, j=8)
                          VB = k.sbuf("VB", [128, 4, 1024], BF16, sA)
                          VBF = VB

                          def cons_fv(blk, p0, pn, ps):
                              k.copy(VBF[:, blk, p0:p0 + pn], ps[:, 0:pn], eng="act" if blk % 2 else "dve")
                          proj_tm(lambda p0, pn: WINK[l][:, :, cols["FV"] + p0:cols["FV"] + p0 + pn], 1024, KC,
                                  lambda kc, blk: HT[:, kc, blk * 128:(blk + 1) * 128], cons_fv)
                          for blk in range(4):
                              k.dma("sp", V(VC_v[:, 4 * tt + blk, :, :], s16),
                                    VBF.v(VBF.t[:, blk, :].rearrange("t (h v) -> t h v", h=8)))
                          ck("A%df" % tt)
                          LFT = k.sbuf("LFT", [128, 4, 8], F32, sA)
                          Wf = wload(WINK[l][:, :, cols["FL"]:cols["FL"] + 8], KC, 8)
                          for blk in range(4):
                              ps = k.ps()
                              for kc in range(KC):
                                  k.mm(ps[:, 0:8], HT[:, kc, blk * 128:(blk + 1) * 128], Wf[:, kc, 0:8],
                                       start=(kc == 0), stop=(kc == KC - 1))
                              k.tt(LFT[:, blk, :], ps[:, 0:8], GN[:, l * GL + G_BF:l * GL + G_BF + 8], ALU.add)
                          k.act(LFT[:, :, :], LFT[:, :, :], AF.Exp, scale=-1.0)
                          k.act(LFT[:, :, :], LFT[:, :, :], AF.Ln, bias=1.0)
                          LF_v = s32.t[R_LF:R_LF + 8, :].rearrange("j (t h) -> t j h", h=8)
                          k.dma("sp", V(LF_v[:, 4 * tt:4 * tt + 4, :], s32), LFT[:, :, :])
                          ck("A%dg" % tt)
                          SPB = k.sbuf("SPB", [128, 4, 512], F32, sA)
                          GAA = k.sbuf("GAA", [17, TT], F32, sA)
                          KTM = k.sbuf("KTM", [128, 4, 512], F32, sA)
                          EN = k.sbuf("EN", [128, 512], F32, sA)
                          KT = k.sbuf("KT", [128, 512], BF16, sA)
                          UB = k.sbuf("UB", [128, 1024], F32, sA)
                          DB = k.sbuf("DB", [128, 4], F32, sA)
                          k.memset(GAA[:, :], 1.0)
                          gla_sp(l, HT, SPB, GAA)
                          gla_v(l, HT, VB)

                          def cons_gk(blk, p0, pn, ps):
                              k.copy(KTM[:, blk, p0:p0 + pn], ps[:, 0:pn], eng="act" if blk % 2 else "dve")
                          proj_tm(lambda p0, pn: WINK[l][:, :, cols["GK"] + p0:cols["GK"] + p0 + pn], 512, KC,
                                  lambda kc, blk: HT[:, kc, blk * 128:(blk + 1) * 128], cons_gk)
                          ck("A%dh" % tt)
                          UX_v = s32.t[R_UX:R_UX + 1024, :].rearrange("(j p) n -> j p n", j=8)
                          DX_v = s32.t[R_DX:R_DX + 4, :].rearrange("a (b f) -> (a b) f", f=4).rearrange(
                              "(j p) f -> j p f", j=8)
                          for blk in range(4):
                              bps = k.ps()
                              k.mm(bps[:, :], TRIS[:, :], SPB[:, blk, :], start=True, stop=True)
                              k.act(EN[:, :], bps[:, :], AF.Exp, scale=-1.0)
                              k.tt(KT[:, :], KTM[:, blk, :], EN[:, :], ALU.mult)
                              dps = k.ps()
                              for h in range(4):
                                  k.mm(dps[:, 2 * h:2 * h + 2], SPB[:, blk, h * 128:(h + 1) * 128], TRIS[:, 126:128],
                                       start=True, stop=True)
                              k.act(DB[:, :], dps.v(dps.t[:, 0:8].rearrange("p (h two) -> p h two", two=2)[:, :, 1]), AF.Exp)
                              for h in range(4):
                                  ups = k.ps()
                                  k.mm(ups[:, 0:256], KT[:, h * 128:(h + 1) * 128], VB[:, blk, h * 256:(h + 1) * 256],
                                       start=True, stop=True)
                                  k.ts(UB[:, h * 256:(h + 1) * 256], ups[:, 0:256], DB[:, h:h + 1], ALU.mult)
                              k.dma("sp", V(UX_v[4 * tt + blk], s32), UB[:, :])
                              k.dma("sp", V(DX_v[4 * tt + blk], s32), DB[:, :])
                          k.barrier()
                          ck("A%d" % tt)

                  if isA:
                      k.stopped = True
                  k.allgather(s32, g32)
                  k.allgather(s16, g16)
                  k.barrier()
                  if isB:
                      k.stopped = False
                  ck("X")

                  sM = contextlib.ExitStack()
                  SOWN = k.sbuf("SOWN", [128, 8, 1024], BF16, sM)
                  NEGF = k.sbuf("NEGF", [128, 512], F32, sM)
                  QCB = k.sbuf("QCB", [128, 64], F32, sM)
                  with contextlib.ExitStack() as sS:
                      LFA = k.sbuf("LFA", [128, 64, 8], F32, sS)
                      TOTB = k.sbuf("TOTB", [128, 64, 8], F32, sS)
                      INCL = k.sbuf("INCL", [128, 64, 8], F32, sS)
                      ZER = k.sbuf("ZER", [128, 64], F32, sS)
                      OWNT = k.sbuf("OWNT", [128, 8, 8, 8], F32, sS)
                      OWNP = k.sbuf("OWNP", [128, 8, 8], F32, sS)
                      ST = k.sbuf("ST", [128, 1024], F32, sS)
                      UBS = [k.sbuf("UBS%d" % i, [128, 1024], F32, sS) for i in range(2)]
                      DBS = [k.sbuf("DBS%d" % i, [128, 4], F32, sS) for i in range(2)]
                      for r in range(8):
                          src = g32.t[r * R32 + R_LF:r * R32 + R_LF + 8, :].rearrange("j (t h) -> t j h", h=8)
                          dst = LFA.t[:, :, :].rearrange("t (j r) h -> t j r h", r=8)[:, :, r, :]
                          k.dma("sp", LFA.v(dst), V(src, g32))
                      k.memset(ZER[:, :], 0.0)
                      lfa2 = LFA.v(LFA.t[:, :, :].rearrange("t b h -> t (b h)"))
                      fs_ps = k.ps()
                      k.mm(fs_ps[:, :], TRIL1[:, :], lfa2, start=True, stop=True)
                      tot_ps = k.ps()
                      k.mm(tot_ps[:, :], ONESF[:, :], lfa2, start=True, stop=True)
                      k.copy(TOTB.v(TOTB.t[:, :, :].rearrange("t b h -> t (b h)")), tot_ps[:, :])
                      for h in range(8):
                          k.op("dve", lambda E, h=h: E.tensor_tensor_scan(
                              out=INCL.t[:, :, h], data0=TOTB.t[:, :, h], data1=ZER.t[:, :], initial=0.0,
                              op0=ALU.add, op1=ALU.add), [TOTB[:, :, :], ZER[:, :]], [INCL[:, :, :]])
                      k.tt(INCL[:, :, :], INCL[:, :, :], TOTB[:, :, :], ALU.subtract)
                      k.tt(NEGF[:, :], fs_ps[:, :], INCL.v(INCL.t[:, :, :].rearrange("t b h -> t (b h)")), ALU.add)
                      tot4 = TOTB.t[:, :, :].rearrange("t (m r) h -> t m h r", r=8)
                      for r in range(8):
                          k.ts(OWNT[:, :, :, r], TOTB.v(tot4[:, :, :, r]), MASKR(r), ALU.mult)
                      k.op("dve", lambda E: E.tensor_reduce(out=OWNP.t[:, :, :], in_=OWNT.t[:, :, :, :], axis=AX.X,
                                                            op=ALU.add), [OWNT[:, :, :, :]], [OWNP[:, :, :]])
                      ex0 = INCL.t[:, :, :].rearrange("t (m r) h -> t m r h", r=8)[:, :, 0, :]
                      k.tt(OWNP[:, :, :], OWNP[:, :, :], INCL.v(ex0), ALU.add)
                      k.ts(QCB.v(QCB.t[:, :].rearrange("t (m h) -> t m h", h=8)), OWNP[:, :, :], -1.0, ALU.mult)
                      k.memset(ST[:, :], 0.0)
                      DXg = lambda r: g32.t[r * R32 + R_DX:r * R32 + R_DX + 4, :].rearrange(
                          "a (b f) -> (a b) f", f=4).rearrange("(j p) f -> j p f", j=8)
                      for b in range(64):
                          J, r = b // 8, b % 8
                          ub = UBS[b % 2]
                          db = DBS[b % 2]
                          k.dma("sp", ub[:, :], V(g32.t[r * R32 + R_UX + J * 128:r * R32 + R_UX + (J + 1) * 128, :], g32))
                          k.dma("sp", db[:, :], V(DXg(r)[J], g32))
                          if r == 0:
                              k.ts(SOWN[:, J, :], ST[:, :], OH(0), ALU.mult)
                          else:
                              k.stt(SOWN[:, J, :], ST[:, :], OH(r), SOWN[:, J, :], ALU.mult, ALU.add)
                          for h in range(4):
                              hs = slice(h * 256, (h + 1) * 256)
                              k.stt(ST[:, hs], ST[:, hs], db[:, h:h + 1], ub[:, hs], ALU.mult, ALU.add)
                      k.barrier()
                      ck("S")

                  for tt in range(2):
                      tsl = slice(tt * TT, (tt + 1) * TT)
                      nJ = 4 * tt + 4
                      with contextlib.ExitStack() as sB:
                          HT = k.sbuf("HT", [128, KC, TT], BF16, sB)
                          OT = k.sbuf("OT", [128, 24, TT], BF16, sB)
                          rmsnorm_x(l, tt, G_MIX, HT)
                          hT = lambda kc: HT[:, kc, :]
                          with contextlib.ExitStack() as sC:
                              CQN = k.sbuf("CQN", [128, 4, TT], BF16, sC)
                              with contextlib.ExitStack() as sC1:
                                  RAW = k.sbuf("RAW", [128, 4, TT], F32, sC1)

                                  def cons_cq(ci, ps, m):
                                      k.copy(RAW[:, ci, :], ps[:, :], eng="act")
                                  proj_fm(WIN[l], C_CQ, 512, KC, hT, cons_cq)
                                  rl = rvb()
                                  rinv_of([(RAW[:, j, :], 128, False) for j in range(4)], 512.0, rl)
                                  for j in range(4):
                                      k.stt(CQN[:, j, :], RAW[:, j, :], gcol(l, G_CQ + j), rl[:, :], ALU.mult, ALU.mult)
                                  k.barrier()
                              QN = [k.sbuf("QN%d" % i, [128, TT], BF16, sC) for i in range(2)]
                              QR = [k.sbuf("QR%d" % i, [128, TT], BF16, sC) for i in range(2)]
                              T64 = [k.sbuf("T64%d" % i, [64, TT], F32, sC) for i in range(2)]
                              KN = [k.sbuf("KN%d" % i, [128, 1024], BF16, sC) for i in range(2)]
                              KRb = [k.sbuf("KRb%d" % i, [128, 1024], BF16, sC) for i in range(2)]
                              for b_ in QR + KRb:
                                  k.memset(b_[:, :], 0.0)
                              VV = [k.sbuf("VV%d" % i, [128, 1024], BF16, sC) for i in range(2)]
                              PT = [k.sbuf("PT%d" % i, [128, TT], BF16, sC) for i in range(3)]
                              SP32 = [k.sbuf("SP32%d" % i, [128, TT], F32, sC) for i in range(2)]
                              FTB = k.sbuf("FTB", [128, TT], F32, sC)
                              RD = k.sbuf("RD", [128, TT], F32, sC)
                              cnt = dict(kv=0, pt=0, sp=0)

                              def attention(kind, h, Qn, Qr, ot_chunk):
                                  rbase = R_KA if kind == "a" else R_KC
                                  vbase = R_VA if kind == "a" else R_VC
                                  kdim = 192 if kind == "a" else 128
                                  ncol = nJ * 128
                                  first = True
                                  for r in range(8):
                                      i3 = cnt["kv"] % 2
                                      cnt["kv"] += 1
                                      kn, krb, vv = KN[i3], KRb[i3], VV[i3]
                                      kbase = r * R16 + rbase + h * kdim
                                      k.dma("sp", kn[:, 0:ncol], V(g16.t[kbase:kbase + 128, 0:ncol], g16))
                                      if kind == "a":
                                          k.dma("sp", krb[0:64, 0:ncol], V(g16.t[kbase + 128:kbase + 192, 0:ncol], g16))
                                      vb0 = r * R16 + vbase + h * 128
                                      k.dma("sp", vv[:, 0:ncol], V(g16.t[vb0:vb0 + 128, 0:ncol], g16))
                                      for J in range(nJ):
                                          m0 = max(J, 4 * tt)
                                          c0 = (m0 - 4 * tt) * 128
                                          N = TT - c0
                                          ks = slice(J * 128, (J + 1) * 128)
                                          S = k.ps()
                                          k.mm(S[:, 0:N], kn[:, ks], Qn[:, c0:TT], start=True, stop=(kind != "a"))
                                          if kind == "a":
                                              k.mm(S[:, 0:N], krb[:, ks], Qr[:, c0:TT], start=False, stop=True)
                                          P = PT[cnt["pt"] % 3]
                                          cnt["pt"] += 1
                                          if kind == "c":
                                              sp = SP32[cnt["sp"] % 2]
                                              cnt["sp"] += 1
                                              k.tt(sp[:, 0:N], S[:, 0:N], FTB[:, c0:TT], ALU.add)
                                              bcol = (8 * J + r) * 8 + h
                                              if J >= 4 * tt:
                                                  k.ts(sp[:, 0:N], sp[:, 0:N], NEGF[:, bcol:bcol + 1], ALU.add, 60.0, ALU.min)
                                                  k.act(P[:, 0:N], sp[:, 0:N], AF.Exp)
                                              else:
                                                  k.act(P[:, 0:N], sp[:, 0:N], AF.Exp, bias=NEGF[:, bcol:bcol + 1])
                                          else:
                                              k.act(P[:, 0:N], S[:, 0:N], AF.Exp)
                                          if J >= 4 * tt:
                                              k.tt(P[:, 0:128], P[:, 0:128], MK[:, r, :], ALU.mult)
                                          last = (r == 7 and J == nJ - 1)
                                          k.mm(PSO[:, c0:TT], vv[:, ks], P[:, 0:N], start=first, stop=last)
                                          k.mm(PSD[:, c0:TT], ONESB[:, :], P[:, 0:N], start=first, stop=last)
                                          first = False
                                  k.recip(RD[:, :], PSD[:, :])
                                  k.tt(ot_chunk, PSO[:, :], RD[:, :], ALU.mult)

                              sc_a = 192.0 ** -0.5
                              for h in range(8):
                                  W = wload(WUQ[l][:, :, h * 192:(h + 1) * 192], 4, 192)
                                  qn_ps = k.ps()
                                  qr_ps = k.ps()
                                  qs_ps = k.ps()
                                  for kc in range(4):
                                      k.mm(qn_ps[:, :], W[:, kc, 0:128], CQN[:, kc, :], start=(kc == 0), stop=(kc == 3))
                                  for kc in range(4):
                                      k.mm(qr_ps[0:64, :], W[:, kc, 128:192], CQN[:, kc, :], start=(kc == 0), stop=(kc == 3))
                                  for kc in range(4):
                                      k.mm(qs_ps[0:32, :], W[:, kc, 160:192], CQN[:, kc, :], start=(kc == 0), stop=(kc == 3))
                                  for kc in range(4):
                                      k.mm(qs_ps[32:64, :], W[:, kc, 128:160], CQN[:, kc, :], start=(kc == 0), stop=(kc == 3))
                                  rh = rvb()
                                  rinv_of([(qn_ps[:, :], 128, False), (qr_ps[0:64, :], 64, False)], 192.0, rh)
                                  k.ts(rh[:, :], rh[:, :], sc_a, ALU.mult)
                                  qn, qr = QN[h % 2], QR[h % 2]
                                  t1, t2 = T64
                                  k.stt(qn[:, :], qn_ps[:, :], gcol(l, G_QN), rh[:, :], ALU.mult, ALU.mult)
                                  k.stt(t1[:, :], qr_ps[0:64, :], gcol(l, G_QR, 64), ROPC[:, tsl], ALU.mult, ALU.mult)
                                  k.stt(t2[:, :], qs_ps[0:64, :], gcol(l, G_QRS, 64), ROPS[:, tsl], ALU.mult, ALU.mult)
                                  k.tt(t1[:, :], t1[:, :], t2[:, :], ALU.add)
                                  k.tt(qr[0:64, :], t1[:, :], rh[0:64, :], ALU.mult)
                                  attention("a", h, qn, qr, OT[:, h, :])
                              sc_c = 128.0 ** -0.5
                              for h in range(8):
                                  W = wload(WIN[l][:, :, C_FQ + h * 128:C_FQ + (h + 1) * 128], KC, 128)
                                  q_ps = k.ps()
                                  for kc in range(KC):
                                      k.mm(q_ps[:, :], W[:, kc, :], hT(kc), start=(kc == 0), stop=(kc == KC - 1))
                                  rh = rvb()
                                  rinv_of([(q_ps[:, :], 128, False)], 128.0, rh)
                                  k.ts(rh[:, :], rh[:, :], sc_c, ALU.mult)
                                  qn = QN[h % 2]
                                  k.stt(qn[:, :], q_ps[:, :], gcol(l, G_FQ), rh[:, :], ALU.mult, ALU.mult)
                                  for mi in range(4):
                                      col = (4 * tt + mi) * 8 + h
                                      k.copy(FTB[:, mi * 128:(mi + 1) * 128], QCB.v(QCB.t[:, col:col + 1].to_broadcast([128, 128])))
                                  attention("c", h, qn, None, OT[:, 16 + h, :])
                              k.barrier()
                              ck("C%d" % tt)
                          with contextlib.ExitStack() as sG:
                              SPB = k.sbuf("SPB", [128, 4, 512], F32, sG)
                              GAA = k.sbuf("GAA", [17, TT], F32, sG)
                              VB = k.sbuf("VB", [128, 4, 1024], BF16, sG)
                              EPH = k.sbuf("EPH", [128, TT], F32, sG)
                              ENH = k.sbuf("ENH", [128, TT], F32, sG)
                              QT = k.sbuf("QT", [128, TT], BF16, sG)
                              KTt = k.sbuf("KTt", [128, TT], BF16, sG)
                              AT = k.sbuf("AT", [128, TT], BF16, sG)
                              OG = k.sbuf("OG", [128, 2, TT], F32, sG)
                              SG = k.sbuf("SG", [128, TT], F32, sG)
                              k.memset(GAA[:, :], 1.0)
                              gla_sp(l, HT, SPB, GAA)
                              gla_v(l, HT, VB)
                              sc_b = 128.0 ** -0.5
                              for h in range(4):
                                  bt = k.ps()
                                  for blk in range(4):
                                      k.mm(bt[:, blk * 128:(blk + 1) * 128], SPB[:, blk, h * 128:(h + 1) * 128], TRIS[:, :],
                                           start=True, stop=True)
                                  k.act(EPH[:, :], bt[:, :], AF.Exp)
                                  k.act(ENH[:, :], bt[:, :], AF.Exp, scale=-1.0)
                                  Wq = wload(WIN[l][:, :, C_GQ + h * 128:C_GQ + (h + 1) * 128], KC, 128)
                                  q_ps = k.ps()
                                  for kc in range(KC):
                                      k.mm(q_ps[:, :], Wq[:, kc, :], hT(kc), start=(kc == 0), stop=(kc == KC - 1))
                                  k.stt(QT[:, :], q_ps[:, :], sc_b, EPH[:, :], ALU.mult, ALU.mult)
                                  Wk2 = wload(WINK[l][:, :, cols["GK"] + h * 128:cols["GK"] + (h + 1) * 128], KC, 128)
                                  k_ps = k.ps()
                                  for kc in range(KC):
                                      k.mm(k_ps[:, :], Wk2[:, kc, :], hT(kc), start=(kc == 0), stop=(kc == KC - 1))
                                  k.tt(KTt[:, :], k_ps[:, :], ENH[:, :], ALU.mult)
                                  a_ps = k.ps()
                                  for blk in range(4):
                                      bs = slice(blk * 128, (blk + 1) * 128)
                                      k.mm(a_ps[:, bs], KTt[:, bs], QT[:, bs], start=True, stop=True)
                                  for blk in range(4):
                                      bs = slice(blk * 128, (blk + 1) * 128)
                                      k.tt(AT[:, bs], a_ps[:, bs], TRI[:, :], ALU.mult)
                                  for half in range(2):
                                      o_ps = k.ps()
                                      vs = slice(h * 256 + half * 128, h * 256 + (half + 1) * 128)
                                      for blk in range(4):
                                          bs = slice(blk * 128, (blk + 1) * 128)
                                          k.mm(o_ps[:, bs], VB[:, blk, vs], AT[:, bs], start=True, stop=False)
                                          k.mm(o_ps[:, bs], SOWN[:, 4 * tt + blk, vs], QT[:, bs], start=False, stop=True)
                                      k.copy(OG[:, half, :], o_ps[:, :], eng="act")
                                  rh = rvb()
                                  rinv_of([(OG[:, 0, :], 128, False), (OG[:, 1, :], 128, False)], 256.0, rh)
                                  Wr = wload(WIN[l][:, :, C_GR + h * 256:C_GR + (h + 1) * 256], KC, 256)
                                  for half in range(2):
                                      g_ps = k.ps()
                                      for kc in range(KC):
                                          k.mm(g_ps[:, :], Wr[:, kc, half * 128:(half + 1) * 128], hT(kc),
                                               start=(kc == 0), stop=(kc == KC - 1))
                                      k.act(SG[:, :], g_ps[:, :], AF.Silu)
                                      k.stt(OG[:, half, :], OG[:, half, :], gcol(l, G_GLA + half), rh[:, :], ALU.mult, ALU.mult)
                                      k.tt(OT[:, 8 + 2 * h + half, :], OG[:, half, :], SG[:, :], ALU.mult)
                              k.barrier()
                              ck("G%d" % tt)
                          with contextlib.ExitStack() as sD:
                              MT = k.sbuf("MT", [128, KC, TT], BF16, sD)
                              SGD = [k.sbuf("SGD%d" % i, [128, TT], F32, sD) for i in range(2)]
                              ACC = [k.sbuf("ACC%d" % i, [128, TT], F32, sD) for i in range(2)]
                              for dp in range(8):
                                  for n in range(3):
                                      Wbn = wload(WBR[l][n][:, :, dp * 256:(dp + 1) * 256], 8, 256)
                                      Wg = wload(WIN[l][:, :, C_GATES + n * D + dp * 256:C_GATES + n * D + (dp + 1) * 256], KC, 256)
                                      for ci in range(2):
                                          dc = dp * 2 + ci
                                          cs = slice(ci * 128, (ci + 1) * 128)
                                          y_ps = k.ps()
                                          for kc in range(8):
                                              k.mm(y_ps[:, :], Wbn[:, kc, cs], OT[:, n * 8 + kc, :], start=(kc == 0), stop=(kc == 7))
                                          g_ps = k.ps()
                                          for kc in range(KC):
                                              k.mm(g_ps[:, :], Wg[:, kc, cs], hT(kc), start=(kc == 0), stop=(kc == KC - 1))
                                          sg = SGD[(n * 2 + ci) % 2]
                                          k.act(sg[:, :], g_ps[:, :], AF.Sigmoid)
                                          acc = ACC[ci]
                                          if n == 0:
                                              k.tt(acc[:, :], y_ps[:, :], sg[:, :], ALU.mult)
                                          else:
                                              k.tt(sg[:, :], y_ps[:, :], sg[:, :], ALU.mult)
                                              if n == 1:
                                                  k.tt(acc[:, :], acc[:, :], sg[:, :], ALU.add)
                                              else:
                                                  k.tt(MT[:, dc, :], acc[:, :], sg[:, :], ALU.add)
                              def cons_out(ci, ps, m):
                                  k.tt(XT[:, ci, tsl], XT[:, ci, tsl], ps[:, :], ALU.add)
                              proj_fm(WOUT[l], 0, D, KC, lambda kc: MT[:, kc, :], cons_out)
                              k.barrier()
                              ck("D%d" % tt)
                  k.barrier()
                  sM.close()
                  with contextlib.ExitStack() as sF:
                      HT2 = k.sbuf("HT2", [128, KC, TOK], BF16, sF)
                      AH = k.sbuf("AH", [128, 22, TOK], BF16, sF)
                      SGF = [k.sbuf("SGF%d" % i, [128, TT], F32, sF) for i in range(2)]
                      for tt in range(2):
                          rmsnorm_x(l, tt, G_FFN, HT2, off=tt * TT)
                      nsg = 0
                      for half in range(2):
                          for hp in range(11):
                              col = (half * 22 + 2 * hp) * 128
                              Wg = wload(WGU[l][:, :, col:col + 256], KC, 256)
                              Wu = wload(WGU[l][:, :, HID + col:HID + col + 256], KC, 256)
                              for ci in range(2):
                                  cs = slice(ci * 128, (ci + 1) * 128)
                                  tsls = [slice(0, TT), slice(TT, 2 * TT)]
                                  g_ps = [k.ps(), k.ps()]
                                  for kc in range(KC):
                                      for tt in range(2):
                                          k.mm(g_ps[tt][:, :], Wg[:, kc, cs], HT2[:, kc, tsls[tt]], start=(kc == 0), stop=(kc == KC - 1))
                                  u_ps = [k.ps(), k.ps()]
                                  for kc in range(KC):
                                      for tt in range(2):
                                          k.mm(u_ps[tt][:, :], Wu[:, kc, cs], HT2[:, kc, tsls[tt]], start=(kc == 0), stop=(kc == KC - 1))
                                  for tt in range(2):
                                      sg = SGF[nsg % 2]
                                      nsg += 1
                                      k.act(sg[:, :], g_ps[tt][:, :], AF.Silu)
                                      k.tt(AH[:, 2 * hp + ci, tsls[tt]], sg[:, :], u_ps[tt][:, :], ALU.mult)
                          for dc in range(KC):
                              W = wload(WDN[l][:, half * 22:(half + 1) * 22, dc * 128:(dc + 1) * 128], 22, 128)
                              tsls = [slice(0, TT), slice(TT, 2 * TT)]
                              d_ps = [k.ps(), k.ps()]
                              for kc in range(22):
                                  for tt in range(2):
                                      k.mm(d_ps[tt][:, :], W[:, kc, :], AH[:, kc, tsls[tt]], start=(kc == 0), stop=(kc == 21))
                              for tt in range(2):
                                  k.tt(XT[:, dc, tsls[tt]], XT[:, dc, tsls[tt]], d_ps[tt][:, :], ALU.add)
                      k.barrier()
                      ck("F")
                  k.barrier()

        except _Stop:
            pass
        k.stopped = False

        if isA:
            o16 = Buf(nc.dram_tensor("s16_out", [R16, 1024], BF16, kind="ExternalOutput"), "s16_out")
            o32 = Buf(nc.dram_tensor("s32_out", [R32, 1024], F32, kind="ExternalOutput"), "s32_out")
            o16.multi = True
            o32.multi = True
            k.barrier()
            for src, dst, nr in ((S16[0], o16, R16), (S32[0], o32, R32)):
                for r0 in range(0, nr, 512):
                    r1 = min(nr, r0 + 512)
                    k.dma("sp", dst[r0:r1, :], src[r0:r1, :], sembuf=dst)
        else:
            k.dma("sp", out_d[:, :, :], XT[:, :, :])
        k.barrier()
    return nc


def _host_consts(c):
    s = np.arange(128)[:, None]
    t = np.arange(128)[None, :]
    tri = (s <= t).astype(np.float32)
    mk = np.zeros((128, 8, 128), np.float32)
    for r in range(8):
        if r < c:
            mk[:, r, :] = 1.0
        elif r == c:
            mk[:, r, :] = tri
    cvec = np.zeros((128, 20), np.float32)
    for r in range(8):
        cvec[:, r] = 1.0 if r < c else 0.0
        cvec[:, 8 + r] = 1.0 if r == c else 0.0
    half = 32
    inv = (np.float32(10000.0) ** (-(np.arange(half, dtype=np.float32) / np.float32(half)))).astype(np.float32)
    cvec[0:64, 16] = np.concatenate([inv, inv])
    cvec[0:64, 17] = np.concatenate([-np.ones(32, np.float32), np.ones(32, np.float32)])
    return dict(
        mk=mk.astype(ml_dtypes.bfloat16), tri=tri.astype(ml_dtypes.bfloat16),
        tris=(tri * np.float32(-1.0 / 16.0)).astype(np.float32), tril1=tri.astype(np.float32), cvec=cvec)


def _pack_gains(inp):
    g = np.zeros((128, DEPTH * GL), np.float32)
    for l in range(DEPTH):
        o = l * GL
        g[:, o + G_MIX:o + G_MIX + 16] = inp["g_mix"][l].reshape(16, 128).T
        g[:, o + G_FFN:o + G_FFN + 16] = inp["g_ffn"][l].reshape(16, 128).T
        g[:, o + G_CQ:o + G_CQ + 4] = inp["g_cq"][l].reshape(4, 128).T
        g[:, o + G_CKV:o + G_CKV + 4] = inp["g_ckv"][l].reshape(4, 128).T
        for nm, cn, cr, cs in (("g_mla_q", G_QN, G_QR, G_QRS), ("g_mla_k", G_KN, G_KR, G_KRS)):
            v = inp[nm][l]
            g[:, o + cn] = v[0:128]
            g[0:64, o + cr] = v[128:192]
            g[0:64, o + cs] = np.concatenate([v[160:192], v[128:160]])
        g[:, o + G_GLA:o + G_GLA + 2] = inp["g_gla_o"][l].reshape(2, 128).T
        g[:, o + G_FQ] = inp["g_fox_q"][l]
        g[:, o + G_FK] = inp["g_fox_k"][l]
        g[:, o + G_BF:o + G_BF + 8] = np.broadcast_to(inp["b_f"][l][None, :], (128, 8))
    return g


_KCOLS = np.concatenate([np.arange(512, 1024), np.arange(1024, 1088), np.arange(1600, 2112), np.arange(2112, 3136),
                         np.arange(3136, 3152), np.arange(5200, 6224), np.arange(6224, 7248), np.arange(7248, 7256)])


def _f32(a):
    return np.ascontiguousarray(a, dtype=np.float32)


def _core_common(inp, c, x_c, lsl):
    gains = _pack_gains(inp)
    if lsl.start:
        gains = np.roll(gains, -lsl.start * GL, axis=1)
    pos = np.asarray(inp["positions"])[0].reshape(8, 8, 128)
    w_a2aug = np.concatenate([inp["w_a2"], inp["b_a"][:, None, :]], axis=1)
    m = dict(gains=_f32(gains), w_a2aug=_f32(w_a2aug[lsl]))
    m.update(_host_consts(c))
    m["xT"] = x_c
    m["pos"] = np.ascontiguousarray(np.broadcast_to(pos[:, c, :].reshape(1, TOK), (64, TOK)).astype(np.int32))
    return m


def _weights_A(inp, lsl):
    return dict(w_ink=_f32(inp["w_in"][lsl][:, :, _KCOLS]), w_ukv=_f32(inp["w_ukv"][lsl]))


def _weights_B(inp, lsl):
    return dict(w_in=_f32(inp["w_in"][lsl]), w_uq=_f32(inp["w_uq"][lsl]), w_branch=_f32(inp["w_branch"][lsl]),
                w_out=_f32(inp["w_out"][lsl]), w_gu=_f32(inp["w_gu"][lsl]), w_down=_f32(inp["w_down"][lsl]))


def _run_fused(nc, x_cores, inp):
    lsl = slice(0, DEPTH)
    wa = _weights_A(inp, lsl)
    wb = _weights_B(inp, lsl)
    shared = dict(wb)
    shared["w_ukv"] = wa["w_ukv"]
    in_maps = []
    for c in range(NCORES):
        m = _core_common(inp, c, x_cores[c], lsl)
        m.update(shared)
        in_maps.append(m)
    res = run_bass_kernel_spmd(nc, in_maps, core_ids=list(range(NCORES)))
    return [np.asarray(res.results[c]["outT"]) for c in range(NCORES)]


def _run_layer(ncA, ncB, x_cores, inp, l):
    lsl = slice(l, l + 1)
    wa = _weights_A(inp, lsl)
    maps = []
    for c in range(NCORES):
        m = _core_common(inp, c, x_cores[c], lsl)
        m.update(wa)
        maps.append(m)
    resA = run_bass_kernel_spmd(ncA, maps, core_ids=list(range(NCORES)))
    g16 = np.ascontiguousarray(np.concatenate([np.asarray(resA.results[c]["s16_out"]) for c in range(NCORES)], axis=0))
    g32 = np.ascontiguousarray(np.concatenate([np.asarray(resA.results[c]["s32_out"]) for c in range(NCORES)], axis=0))
    wb = _weights_B(inp, lsl)
    maps = []
    for c in range(NCORES):
        m = _core_common(inp, c, x_cores[c], lsl)
        m.update(wb)
        m["g16_0"] = g16
        m["g32_0"] = g32
        maps.append(m)
    resB = run_bass_kernel_spmd(ncB, maps, core_ids=list(range(NCORES)))
    return [np.asarray(resB.results[c]["outT"]) for c in range(NCORES)]


def _to_cores(x):
    x = x.reshape(8, 8, 128, D)
    out = []
    for c in range(NCORES):
        xs = x[:, c].reshape(TOK, D)
        out.append(np.ascontiguousarray(xs.T.reshape(KC, 128, TOK).transpose(1, 0, 2)))
    return out


def _from_cores(x_cores):
    out = np.empty((8, 8, 128, D), np.float32)
    for c in range(NCORES):
        xs = x_cores[c].transpose(1, 0, 2).reshape(D, TOK).T
        out[:, c] = xs.reshape(8, 128, D)
    return out.reshape(1, 8192, D)


def kernel(**inputs):
    inp = {k_: np.asarray(v) for k_, v in inputs.items()}
    x_cores = _to_cores(inp["x"][0])
    if FUSED:
        nc = build(DEPTH, "fused")
        x_cores = _run_fused(nc, x_cores, inp)
    else:
        ncA = build(1, "A")
        ncB = build(1, "B")
        for l in range(DEPTH):
            x_cores = _run_layer(ncA, ncB, x_cores, inp, l)
    return _from_cores(x_cores)
```
